# Optimizing a Trainium2 kernel written in Bass

```python
import jax, jax.numpy as jnp
from jax import lax
import numpy as np

D_MODEL = 2048
BATCH = 4
SEQ = 2048
DEPTH = 4

N_MIXERS = 2
N_RET_LAYERS = (DEPTH + 1) // 2
N_MLA_LAYERS = DEPTH // 2

RET_QK_DIM = 256
RET_HEADS = D_MODEL // RET_QK_DIM
RET_V_DIM = 2 * RET_QK_DIM
RET_QK = RET_HEADS * RET_QK_DIM
RET_VD = RET_HEADS * RET_V_DIM
RET_IN = 2 * RET_QK + 2 * RET_VD
RET_CHUNK = 128

MLA_NOPE = 128
MLA_ROPE = 64
MLA_V = 128
MLA_HEADS = D_MODEL // MLA_V
MLA_Q_LORA = D_MODEL // 4
MLA_KV_LORA = 512
MLA_IN = MLA_Q_LORA + MLA_KV_LORA + MLA_ROPE
MLA_Q_BLOCK = 128

D_FF = -(-8 * D_MODEL // (3 * 256)) * 256

ROPE_THETA = 10000.0
NORM_EPS = 1e-6
POS_OFFSET_MAX = 4096

kernel_name = "hybrid_retention_mla_swiglu_sandwich"


def rms_norm(x, g):
    xf = x.astype(jnp.float32)
    y = xf * lax.rsqrt(jnp.mean(xf * xf, axis=-1, keepdims=True) + NORM_EPS)
    return (y * g.astype(jnp.float32)).astype(x.dtype)


def rope(x, positions):
    d = x.shape[-1]
    inv = ROPE_THETA ** (-jnp.arange(0, d, 2, dtype=jnp.float32) / d)
    ang = positions.astype(jnp.float32)[..., None] * inv
    cos = jnp.cos(ang)[:, :, None, :]
    sin = jnp.sin(ang)[:, :, None, :]
    xf = x.astype(jnp.float32)
    x1, x2 = xf[..., : d // 2], xf[..., d // 2:]
    out = jnp.concatenate([x1 * cos - x2 * sin, x1 * sin + x2 * cos], axis=-1)
    return out.astype(x.dtype)


def retention(x, positions, w_in, gn_g, w_out):
    B, S, _ = x.shape
    H, dk, dv, C = RET_HEADS, RET_QK_DIM, RET_V_DIM, RET_CHUNK
    n_chunks = S // C
    proj = x @ w_in
    q, k, v, g = jnp.split(proj, [RET_QK, 2 * RET_QK, 2 * RET_QK + RET_VD], axis=-1)
    q = rope(q.reshape(B, S, H, dk), positions)
    k = rope(k.reshape(B, S, H, dk), positions) * (dk ** -0.5)
    v = v.reshape(B, S, H, dv)

    def to_chunks(t):
        return t.astype(jnp.float32).reshape(B, n_chunks, C, H, -1).transpose(1, 0, 3, 2, 4)

    qc, kc, vc = to_chunks(q), to_chunks(k), to_chunks(v)

    log_gamma = jnp.log1p(-jnp.exp2(-5.0 - jnp.arange(H, dtype=jnp.float32)))
    idx = jnp.arange(C, dtype=jnp.float32)
    rel = idx[:, None] - idx[None, :]
    causal = rel >= 0
    decay_in = jnp.where(causal, jnp.exp(log_gamma[:, None, None] * jnp.where(causal, rel, 0.0)), 0.0)
    decay_q = jnp.exp(log_gamma[:, None] * (idx + 1.0))[..., None]
    decay_k = jnp.exp(log_gamma[:, None] * (C - 1.0 - idx))[..., None]
    decay_chunk = jnp.exp(log_gamma * C)[:, None, None]

    def step(state, inp):
        qi, ki, vi = inp
        scores = jnp.einsum('bhqd,bhkd->bhqk', qi, ki) * decay_in
        y = (jnp.einsum('bhqk,bhkv->bhqv', scores, vi)
             + jnp.einsum('bhqd,bhdv->bhqv', qi, state) * decay_q)
        state = state * decay_chunk + jnp.einsum('bhkd,bhkv->bhdv', ki * decay_k, vi)
        return state, y

    state0 = jnp.zeros((B, H, dk, dv), jnp.float32)
    _, y = lax.scan(step, state0, (qc, kc, vc))
    y = y.transpose(1, 0, 3, 2, 4).reshape(B, S, H, dv)
    mu = jnp.mean(y, axis=-1, keepdims=True)
    var = jnp.mean(jnp.square(y - mu), axis=-1, keepdims=True)
    y = ((y - mu) * lax.rsqrt(var + NORM_EPS)).reshape(B, S, RET_VD) * gn_g.astype(jnp.float32)
    out = (jax.nn.silu(g.astype(jnp.float32)) * y).astype(x.dtype)
    return out @ w_out


def mla(x, positions, w_in, g_q, g_kv, w_uq, w_ukv, w_out):
    B, S, _ = x.shape
    H = MLA_HEADS
    dq = MLA_NOPE + MLA_ROPE
    c = x @ w_in
    c_q, c_kv, k_r = jnp.split(c, [MLA_Q_LORA, MLA_Q_LORA + MLA_KV_LORA], axis=-1)
    c_q = rms_norm(c_q, g_q)
    c_kv = rms_norm(c_kv, g_kv)
    q = (c_q @ w_uq).reshape(B, S, H, dq)
    q = jnp.concatenate([q[..., :MLA_NOPE], rope(q[..., MLA_NOPE:], positions)], axis=-1) * (dq ** -0.5)
    kv = (c_kv @ w_ukv).reshape(B, S, H, MLA_NOPE + MLA_V)
    k_nope, v = kv[..., :MLA_NOPE], kv[..., MLA_NOPE:]
    k_rope = rope(k_r[:, :, None, :], positions)
    k = jnp.concatenate([k_nope, jnp.broadcast_to(k_rope, (B, S, H, MLA_ROPE))], axis=-1)

    n_blocks = S // MLA_Q_BLOCK
    qb = q.reshape(B, n_blocks, MLA_Q_BLOCK, H, dq).transpose(1, 0, 2, 3, 4)
    kpos = jnp.arange(S)

    def attend_block(args):
        qi, bi = args
        s = jnp.einsum('bqhd,bkhd->bhqk', qi, k).astype(jnp.float32)
        qpos = bi * MLA_Q_BLOCK + jnp.arange(MLA_Q_BLOCK)
        s = jnp.where(kpos[None, :] <= qpos[:, None], s, -jnp.inf)
        p = jax.nn.softmax(s, axis=-1).astype(v.dtype)
        return jnp.einsum('bhqk,bkhv->bqhv', p, v)

    o = lax.map(attend_block, (qb, jnp.arange(n_blocks)))
    o = o.transpose(1, 0, 2, 3, 4).reshape(B, S, H * MLA_V)
    return o @ w_out


def swiglu(x, w_gu, w_down):
    gate, up = jnp.split(x @ w_gu, [D_FF], axis=-1)
    return (jax.nn.silu(gate) * up) @ w_down


def setup_inputs(seed: int = 0) -> dict:
    key = jax.random.key(seed)
    ks = jax.random.split(key, 17)
    f32 = jnp.float32

    def w(k, shape, fan_in):
        return jax.random.normal(k, shape, f32) * (fan_in ** -0.5)

    def gain(k, shape):
        return 1.0 + 0.02 * jax.random.normal(k, shape, f32)

    x = jax.random.normal(ks[0], (BATCH, SEQ, D_MODEL), f32)
    positions = (jnp.arange(SEQ, dtype=jnp.int32)[None, :]
                 + jax.random.randint(ks[1], (BATCH, 1), 0, POS_OFFSET_MAX, dtype=jnp.int32))
    return {
        "x": x,
        "positions": positions,
        "norm_mix_pre": gain(ks[2], (DEPTH, D_MODEL)),
        "norm_mix_post": gain(ks[3], (DEPTH, D_MODEL)),
        "norm_ffn_pre": gain(ks[4], (DEPTH, D_MODEL)),
        "norm_ffn_post": gain(ks[5], (DEPTH, D_MODEL)),
        "ret_w_in": w(ks[6], (N_RET_LAYERS, D_MODEL, RET_IN), D_MODEL),
        "ret_gn_g": gain(ks[7], (N_RET_LAYERS, RET_VD)),
        "ret_w_out": w(ks[8], (N_RET_LAYERS, RET_VD, D_MODEL), RET_VD),
        "mla_w_in": w(ks[9], (N_MLA_LAYERS, D_MODEL, MLA_IN), D_MODEL),
        "mla_g_q": gain(ks[10], (N_MLA_LAYERS, MLA_Q_LORA)),
        "mla_g_kv": gain(ks[11], (N_MLA_LAYERS, MLA_KV_LORA)),
        "mla_w_uq": w(ks[12], (N_MLA_LAYERS, MLA_Q_LORA, MLA_HEADS * (MLA_NOPE + MLA_ROPE)), MLA_Q_LORA),
        "mla_w_ukv": w(ks[13], (N_MLA_LAYERS, MLA_KV_LORA, MLA_HEADS * (MLA_NOPE + MLA_V)), MLA_KV_LORA),
        "mla_w_out": w(ks[14], (N_MLA_LAYERS, MLA_HEADS * MLA_V, D_MODEL), MLA_HEADS * MLA_V),
        "ffn_w_gu": w(ks[15], (DEPTH, D_MODEL, 2 * D_FF), D_MODEL),
        "ffn_w_down": w(ks[16], (DEPTH, D_FF, D_MODEL), D_FF),
    }


def reference(x, positions, norm_mix_pre, norm_mix_post, norm_ffn_pre, norm_ffn_post,
              ret_w_in, ret_gn_g, ret_w_out,
              mla_w_in, mla_g_q, mla_g_kv, mla_w_uq, mla_w_ukv, mla_w_out,
              ffn_w_gu, ffn_w_down):
    for i in range(DEPTH):
        j = i // N_MIXERS
        h = rms_norm(x, norm_mix_pre[i])
        if i % N_MIXERS == 0:
            h = retention(h, positions, ret_w_in[j], ret_gn_g[j], ret_w_out[j])
        else:
            h = mla(h, positions, mla_w_in[j], mla_g_q[j], mla_g_kv[j],
                    mla_w_uq[j], mla_w_ukv[j], mla_w_out[j])
        x = x + rms_norm(h, norm_mix_post[i])
        h = rms_norm(x, norm_ffn_pre[i])
        x = x + rms_norm(swiglu(h, ffn_w_gu[i], ffn_w_down[i]), norm_ffn_post[i])
    return x
```

```python
import numpy as np
import ml_dtypes
import concourse.bass as bass
import concourse.mybir as mybir
from concourse.bass_utils import run_bass_kernel_spmd

F32 = mybir.dt.float32
BF16 = mybir.dt.bfloat16
I32 = mybir.dt.int32
ALU = mybir.AluOpType
AF = mybir.ActivationFunctionType

D = 2048
S = 2048
NB = 4
L = 4
G = 512
NG = S // G
KC = D // 128
DFF = 5632
NF = DFF // 128
RH, RDK, RDV, RC = 8, 256, 512, 128
MH, MN, MR, MV, MQL, MKL = 16, 128, 64, 128, 512, 512
EPS = 1e-6
THETA = 10000.0
PI = float(np.pi)
SLOT = 4096
NSLOT = 3
SAME_ENGINE_SYNC = True


class Buf:
    __slots__ = ("name", "w", "r", "dsem")

    def __init__(self, name):
        self.name = name
        self.w = None
        self.r = {}
        self.dsem = None


class Eng:
    def __init__(self, name, is_pe=False):
        self.name = name
        self.q = []
        self.cnt = 0
        self.seen = {}
        self.is_pe = is_pe


class Prog:
    def __init__(self, nc):
        self.nc = nc
        self.pe = Eng("pe", True)
        self.act = Eng("act")
        self.dve = Eng("dve")
        self.pool = Eng("pool")
        self.sp = Eng("sp")
        self.engs = [self.pe, self.act, self.dve, self.pool, self.sp]
        self.dma_sems = {}
        self.nd = 0

    def _deps(self, eng, R, W):
        deps = {}

        def add(d):
            if d is None:
                return
            k, v = d
            if deps.get(k, 0) < v:
                deps[k] = v
        for b in R:
            add(b.w)
        for b in W:
            add(b.w)
            for k, v in b.r.items():
                add((k, v))
        for k, v in deps.items():
            if k == eng.name and (eng.is_pe or not SAME_ENGINE_SYNC):
                continue
            if eng.seen.get(k, 0) < v:
                eng.q.append(("wait", k, v))
                eng.seen[k] = v

    def op(self, eng, fn, R=(), W=()):
        self._deps(eng, R, W)
        eng.cnt += 1
        me = (eng.name, eng.cnt)
        eng.q.append(("op", fn))
        for b in R:
            if b.r.get(me[0], 0) < me[1]:
                b.r[me[0]] = me[1]
        for b in W:
            b.w = me
            b.r = {}

    def dma(self, eng, out_ap, in_ap, R=(), W=(), sembuf=None):
        self._deps(eng, R, W)
        sb = sembuf if sembuf is not None else (W[0] if W else R[0])
        if sb.dsem is None:
            sb.dsem = "d%d" % self.nd
            self.nd += 1
            self.dma_sems[sb.dsem] = 0
        self.dma_sems[sb.dsem] += 16
        me = (sb.dsem, self.dma_sems[sb.dsem])
        eng.q.append(("dma", out_ap, in_ap, sb.dsem))
        for b in R:
            if b.r.get(me[0], 0) < me[1]:
                b.r[me[0]] = me[1]
        for b in W:
            b.w = me
            b.r = {}

    def barrier_wait(self, eng, bufs):
        self._deps(eng, (), bufs)


def build(nlayers=L, ngroups=NG, dbg=None):
    nc = bass.Bass("TRN2", target_bir_lowering=False)
    P = Prog(nc)
    pe, act, dve, pool, sp = P.pe, P.act, P.dve, P.pool, P.sp

    def din(name, shape, dt=F32):
        return nc.dram_tensor(name, list(shape), dt, kind="ExternalInput")
    xT_d = din("xT", [KC, 128, S])
    pos_d = din("pos", [1, S], I32)
    gains_d = din("gains", [128, 16 * KC])
    gn_d = din("gn", [128, 64])
    gqkv_d = din("gqkv", [128, 16])
    cf_d = din("cf", [128, 8 * 128 + 8 + 8 + 4 + 128])
    cb_d = din("cb", [128, 128 + 128 + 4 * 512], BF16)
    ret_w_in_d = din("ret_w_in", [2, D, 12288])
    ret_w_out_d = din("ret_w_out", [2, 4096, D])
    mla_w_in_d = din("mla_w_in", [2, D, 1152])
    mla_w_uq_d = din("mla_w_uq", [2, MQL, 4096])
    mla_w_ukv_d = din("mla_w_ukv", [2, MKL, 4096])
    mla_w_out_d = din("mla_w_out", [2, D, D])
    ffn_w_gu_d = din("ffn_w_gu", [L, D, 2 * DFF])
    ffn_w_down_d = din("ffn_w_down", [L, DFF, D])
    outT_d = nc.dram_tensor("outT", [KC, 128, S], F32, kind="ExternalOutput")
    xs_d = nc.dram_tensor("xs", [KC, 128, S], F32)
    ssc_d = nc.dram_tensor("ssc", [RH, 128, 2, RDV], F32)
    xs_bufs = [Buf("xs%d" % g) for g in range(NG)]
    ssc_bufs = [Buf("ssc%d" % h) for h in range(RH)]
    out_bufs = [Buf("out%d" % g) for g in range(NG)]

    from contextlib import ExitStack
    es = ExitStack()

    def sb(name, shape, dt):
        return es.enter_context(nc.sbuf_tensor("sb_" + name, list(shape), dt))

    def ps(name, shape, dt):
        return es.enter_context(nc.psum_tensor(name, list(shape), dt))

    xg = sb("xg", [128, KC, G], F32)
    hoT = sb("hoT", [128, 2 * KC * G], BF16)
    big = sb("big", [128, NF, G], BF16)
    ring = sb("ring", [128, NSLOT, SLOT], BF16)
    arena = sb("arena", [128, 40 * 512], BF16)
    sq = sb("sq", [128, 2, G], F32)
    rstd = sb("rstd", [128, 2, G], F32)
    tabs = sb("tabs", [128, 4, G], F32)
    posi = sb("posi", [128, G], I32)
    posf = sb("posf", [128, G], F32)
    gains = sb("gains", [128, 16 * KC], F32)
    gn = sb("gn", [128, 64], F32)
    gqkv = sb("gqkv", [128, 16], F32)
    cf = sb("cf", [128, 8 * 128 + 8 + 8 + 4 + 128], F32)
    cb = sb("cb", [128, 128 + 128 + 4 * 512], BF16)
    small = sb("small", [128, 16], F32)

    hT = hoT[:, 0:KC * G].rearrange("p (c t) -> p c t", c=KC)
    oT = hoT.bitcast(F32).rearrange("p (c t) -> p c t", c=KC) if hasattr(hoT, "bitcast") else None
    DinT = cf[:, 0:1024].rearrange("p (h q) -> p h q", h=RH)
    dq_c = cf[:, 1024:1032]
    dk_c = cf[:, 1032:1040]
    inv_ret = cf[:, 1040:1041]
    inv_mla = cf[:, 1041:1042]
    sgn_mla = cf[:, 1042:1043]
    ones_f = cf[:, 1044:1044 + 128]
    ident_b = cb[:, 0:128]
    ones_b = cb[:, 128:256]
    maskF = cb[:, 256:256 + 2048].rearrange("p (j q) -> p j q", j=4)

    banks = [ps("pb%d" % i, [128, 512], F32) for i in range(7)]
    tpb = ps("tpb", [128, 1024], BF16)
    bankB = [Buf("bank%d" % i) for i in range(7)]
    tpA_B, tpB_B = Buf("tpA"), Buf("tpB")
    acc_i = [0]

    def acc():
        i = acc_i[0] % 4
        acc_i[0] += 1
        return banks[i], bankB[i]
    ssP, ssB = banks[4], bankB[4]
    a1P, a1B = banks[5], bankB[5]
    a2P, a2B = banks[6], bankB[6]
    tp2 = banks[5][:, :].bitcast(BF16)

    REG = {}

    def PB(name):
        if name not in REG:
            REG[name] = Buf(name)
        return REG[name]
    xgB = [Buf("xg%d" % c) for c in range(KC)]
    hTB = [Buf("hT%d" % c) for c in range(KC)]
    bigB = [Buf("big%d" % f) for f in range(NF)]
    ringB = [Buf("ring%d" % i) for i in range(NSLOT)]
    sqB = [Buf("sq0"), Buf("sq1")]
    rstdB = [Buf("rstd0"), Buf("rstd1")]
    tabsB = Buf("tabs")
    posB = Buf("pos")
    constB = Buf("const")
    smallB = Buf("small")

    oTB_extra = [Buf("oTx%d" % j) for j in range(KC)]

    def oT_bufs(j):
        if j < 8:
            return [hTB[2 * j], hTB[2 * j + 1]]
        return [oTB_extra[j]]

    def oT_ap(j):
        return hoT[:, j * 1024:(j + 1) * 1024].bitcast(F32)

    ring_i = [0]

    def get_slab(dram2d, r0, nrows, col_ranges):
        i = ring_i[0] % NSLOT
        ring_i[0] += 1
        kc = nrows // 128
        ncols = sum(n for _, n in col_ranges)
        assert kc * ncols <= SLOT
        view = ring[:, i, 0:kc * ncols].rearrange("p (k n) -> p k n", k=kc)
        off = 0
        for (c0, n) in col_ranges:
            src = dram2d[r0:r0 + nrows, c0:c0 + n].rearrange("(k p) n -> p k n", p=128)
            P.dma(pool, view[:, :, off:off + n], src, W=[ringB[i]])
            off += n
        return view, ringB[i]

    def mm(out_ap, outB, pairs, R, start=True, stop=True):
        def fn(e, out_ap=out_ap, pairs=pairs, start=start, stop=stop):
            ins = None
            n = len(pairs)
            for i, (l, r) in enumerate(pairs):
                ins = e.matmul(out_ap, l, r, start=(start and i == 0), stop=(stop and i == n - 1))
            return ins
        P.op(pe, fn, R=R, W=[outB])

    def tr(out_ap, outB, in_ap, R):
        P.op(pe, lambda e: e.transpose(out_ap, in_ap, ident_b), R=list(R) + [constB], W=[outB])

    def A(fn, R, W):
        P.op(act, fn, R=R, W=W)

    def V(fn, R, W):
        P.op(dve, fn, R=R, W=W)

    for dst, src in ((gains, gains_d), (gn, gn_d), (gqkv, gqkv_d), (cf, cf_d), (cb, cb_d)):
        P.dma(sp, dst[:, :], src[:, :], W=[constB])

    def gain_ap(kind, l, c):
        i = (kind * 4 + l) * KC + c
        return gains[:, i:i + 1]

    def stats_rstd(n_feat, k, srcP=None, srcB=None):
        srcP = ssP if srcP is None else srcP
        srcB = ssB if srcB is None else srcB
        A(lambda e: e.activation(rstd[:, k, :], srcP[:, :], AF.Sqrt, bias=EPS, scale=1.0 / n_feat),
          R=[srcB], W=[rstdB[k]])
        V(lambda e: e.reciprocal(rstd[:, k, :], rstd[:, k, :]), R=[rstdB[k]], W=[rstdB[k]])

    def sq_accum(src_ap, srcB, idx, first, last, ssP_=None, ssB_=None):
        ssP_ = ssP if ssP_ is None else ssP_
        ssB_ = ssB if ssB_ is None else ssB_
        k = idx % 2
        A(lambda e: e.activation(sq[:, k, :], src_ap, AF.Square), R=[srcB], W=[sqB[k]])
        mm(ssP_[:, :], ssB_, [(ones_f, sq[:, k, :])], R=[sqB[k], constB], start=first, stop=last)

    def pre_norm(kind, l):
        for c in range(KC):
            sq_accum(xg[:, c, :], xgB[c], c, c == 0, c == KC - 1)
        stats_rstd(D, 0)
        for c in range(KC):
            V(lambda e, c=c: e.scalar_tensor_tensor(hT[:, c, :], xg[:, c, :], gain_ap(kind, l, c),
                                                    rstd[:, 0, :], ALU.mult, ALU.mult),
              R=[xgB[c], rstdB[0], constB], W=[hTB[c]])

    def evac_out(j, psP, psB):
        A(lambda e: e.activation(oT_ap(j), psP[:, :], AF.Copy), R=[psB], W=oT_bufs(j))
        sq_accum(psP[:, :], psB, j, j == 0, j == KC - 1)

    def post_norm(kind, l):
        stats_rstd(D, 0)
        for c in range(KC):
            V(lambda e, c=c: e.scalar_tensor_tensor(oT_ap(c), oT_ap(c), gain_ap(kind, l, c),
                                                    rstd[:, 0, :], ALU.mult, ALU.mult),
              R=oT_bufs(c) + [rstdB[0], constB], W=oT_bufs(c))
            V(lambda e, c=c: e.tensor_tensor(xg[:, c, :], xg[:, c, :], oT_ap(c), ALU.add),
              R=oT_bufs(c) + [xgB[c]], W=[xgB[c]])

    def ffn(l):
        w2 = ffn_w_gu_d[l]
        for f in range(NF):
            slab, sB = get_slab(w2, 0, D, [(f * 128, 128), (DFF + f * 128, 128)])
            gP, gB = acc()
            mm(gP[:, :], gB, [(slab[:, kc, 0:128], hT[:, kc, :]) for kc in range(KC)], R=[sB] + hTB)
            uP, uB = acc()
            mm(uP[:, :], uB, [(slab[:, kc, 128:256], hT[:, kc, :]) for kc in range(KC)], R=[sB] + hTB)
            k = f % 2
            A(lambda e, gP=gP, k=k: e.activation(sq[:, k, :], gP[:, :], AF.Silu), R=[gB], W=[sqB[k]])
            V(lambda e, uP=uP, k=k, f=f: e.tensor_tensor(big[:, f, :], sq[:, k, :], uP[:, :], ALU.mult),
              R=[sqB[k], uB], W=[bigB[f]])
        wd = ffn_w_down_d[l]
        for j in range(KC):
            slA, bA = get_slab(wd, 0, 2816, [(j * 128, 128)])
            slB, bB = get_slab(wd, 2816, 2816, [(j * 128, 128)])
            oP, oB = acc()
            pairs = [(slA[:, kc, :], big[:, kc, :]) for kc in range(22)] + \
                    [(slB[:, kc, :], big[:, 22 + kc, :]) for kc in range(22)]
            mm(oP[:, :], oB, pairs, R=[bA, bB] + bigB)
            evac_out(j, oP, oB)

    cosT = tabs[:, 0, :]
    sinT = tabs[:, 1, :]
    t1 = tabs[:, 2, :]
    t2 = tabs[:, 3, :]
    tmpB = [Buf("t1"), Buf("t2")]

    def make_tables(g, mla):
        P.dma(sp, posi[:, :], pos_d[0:1, g * G:(g + 1) * G].partition_broadcast(128), W=[posB])
        V(lambda e: e.tensor_copy(posf[:, :], posi[:, :]), R=[posB], W=[posB])
        inv = inv_mla if mla else inv_ret
        for ti, shift in ((1, 0.0), (0, 0.25)):
            V(lambda e, ti=ti, shift=shift: e.tensor_scalar(tabs[:, ti, :], posf[:, :], inv, shift, ALU.mult, ALU.add),
              R=[posB, constB], W=[tabsB])
            V(lambda e, ti=ti: e.tensor_copy(posi[:, :], tabs[:, ti, :]), R=[tabsB], W=[posB])
            V(lambda e: e.tensor_copy(t1, posi[:, :]), R=[posB], W=[tmpB[0]])
            V(lambda e, ti=ti: e.tensor_tensor(tabs[:, ti, :], tabs[:, ti, :], t1, ALU.subtract), R=[tabsB, tmpB[0]], W=[tabsB])
            A(lambda e, ti=ti: e.activation(tabs[:, ti, :], tabs[:, ti, :], AF.Sin, scale=2 * PI), R=[tabsB], W=[tabsB])
        if mla:
            V(lambda e: e.tensor_scalar(tabs[:, 1, :], tabs[:, 1, :], sgn_mla, None, ALU.mult),
              R=[tabsB, constB], W=[tabsB])

    def retention(l, g):
        j = l // 2
        w_in = ret_w_in_d[j]
        o = [0]

        def carve(n_bf16):
            a = arena[:, o[0]:o[0] + n_bf16]
            o[0] += n_bf16
            return a
        qT = carve(2 * G).rearrange("p (c t) -> p c t", c=2)
        kT = carve(2 * G).rearrange("p (c t) -> p c t", c=2)
        vT = carve(4 * RDV).rearrange("p (c t) -> p c t", c=4)
        gT = carve(4 * RDV).rearrange("p (c t) -> p c t", c=4)
        ktok = carve(256)
        sTb = carve(128)
        Sbf = carve(2 * RDV).rearrange("p (c t) -> p c t", c=2)
        yg = carve(RDV)
        S32 = [carve(2 * 2 * RDV).bitcast(F32).rearrange("p (c t) -> p c t", c=2) for _ in range(2)]
        y32 = carve(2 * RDV).bitcast(F32)
        i32 = carve(2 * RDV).bitcast(F32)
        st6 = small[:, 0:6]
        mv = small[:, 6:8]
        rs = small[:, 8:9]
        qTB, kTB = PB("r_qT"), PB("r_kT")
        vTB = [PB("r_v%d" % i) for i in range(4)]
        gTB = [PB("r_g%d" % i) for i in range(4)]
        ktokB, sTbB, SbfB, ygB, y32B, i32B = PB("r_ktok"), PB("r_sTb"), PB("r_Sbf"), PB("r_yg"), PB("r_y32"), PB("r_i32")
        S32B = [PB("r_S32a"), PB("r_S32b")]
        if g == 0:
            for eng in (pe, act, dve, sp):
                P.barrier_wait(eng, [b for n, b in REG.items() if n.startswith("m_")])
        gC = [float(np.exp(np.float32(np.log1p(-np.exp2(np.float32(-5.0 - h)))) * np.float32(RC))) for h in range(RH)]

        for h in range(RH):
            Sh, ShB = S32[h % 2], S32B[h % 2]
            if g == 0:
                V(lambda e, Sh=Sh: e.memset(Sh[:, :, :], 0.0), R=[], W=[ShB])
            else:
                P.dma(sp, Sh[:, :, :], ssc_d[h], R=[ssc_bufs[h]], W=[ShB])
            A(lambda e, Sh=Sh: e.activation(Sbf[:, :, :], Sh[:, :, :], AF.Copy), R=[ShB], W=[SbfB])

            for which, base, dst, dstB in ((0, h * 256, qT, qTB), (1, 2048 + h * 256, kT, kTB)):
                slab, sB = get_slab(w_in, 0, D, [(base, 256)])
                p0, b0 = acc()
                mm(p0[:, :], b0, [(slab[:, kc, 0:128], hT[:, kc, :]) for kc in range(KC)], R=[sB] + hTB)
                p1, b1 = acc()
                mm(p1[:, :], b1, [(slab[:, kc, 128:256], hT[:, kc, :]) for kc in range(KC)], R=[sB] + hTB)
                V(lambda e, p0=p0: e.tensor_tensor(t1, p0[:, :], cosT, ALU.mult), R=[b0, tabsB], W=[tmpB[0]])
                V(lambda e, p1=p1: e.tensor_tensor(t2, p1[:, :], sinT, ALU.mult), R=[b1, tabsB], W=[tmpB[1]])
                V(lambda e, dst=dst: e.tensor_tensor(dst[:, 0, :], t1, t2, ALU.subtract), R=tmpB, W=[dstB])
                V(lambda e, p0=p0: e.tensor_tensor(t1, p0[:, :], sinT, ALU.mult), R=[b0, tabsB], W=[tmpB[0]])
                V(lambda e, p1=p1: e.tensor_tensor(t2, p1[:, :], cosT, ALU.mult), R=[b1, tabsB], W=[tmpB[1]])
                V(lambda e, dst=dst: e.tensor_tensor(dst[:, 1, :], t1, t2, ALU.add), R=tmpB, W=[dstB])
            for which, base, dst, dstBs in ((0, 4096 + h * 512, vT, vTB), (1, 8192 + h * 512, gT, gTB)):
                for half in range(2):
                    slab, sB = get_slab(w_in, 0, D, [(base + half * 256, 256)])
                    for tt in range(4):
                        p0, b0 = acc()
                        mm(p0[:, 0:256], b0, [(hT[:, kc, tt * 128:(tt + 1) * 128], slab[:, kc, :]) for kc in range(KC)],
                           R=[sB] + hTB)
                        fn = AF.Copy if which == 0 else AF.Silu
                        A(lambda e, p0=p0, dst=dst, tt=tt, half=half, fn=fn:
                          e.activation(dst[:, tt, half * 256:(half + 1) * 256], p0[:, 0:256], fn),
                          R=[b0], W=[dstBs[tt]])
            for c in range(4):
                tok = slice(c * 128, (c + 1) * 128)
                sP, sBk = acc()
                mm(sP[:, 0:128], sBk, [(kT[:, dc, tok], qT[:, dc, tok]) for dc in range(2)], R=[kTB, qTB])
                V(lambda e, sP=sP, h=h: e.tensor_tensor(sTb, sP[:, 0:128], DinT[:, h, :], ALU.mult),
                  R=[sBk, constB], W=[sTbB])
                iaP, iaB = acc()
                mm(iaP[:, :], iaB, [(sTb, vT[:, c, :])], R=[sTbB, vTB[c]])
                ieP, ieB = acc()
                mm(ieP[:, :], ieB, [(qT[:, dc, tok], Sbf[:, dc, :]) for dc in range(2)], R=[qTB, SbfB])
                A(lambda e, ieP=ieP, h=h: e.activation(i32, ieP[:, :], AF.Copy, scale=dq_c[:, h:h + 1]),
                  R=[ieB, constB], W=[i32B])
                V(lambda e, iaP=iaP: e.tensor_tensor(y32, i32, iaP[:, :], ALU.add), R=[i32B, iaB], W=[y32B])
                V(lambda e: e.bn_stats(st6, y32), R=[y32B], W=[smallB])
                V(lambda e: e.bn_aggr(mv, st6), R=[smallB], W=[smallB])
                A(lambda e: e.activation(rs, mv[:, 1:2], AF.Sqrt, bias=EPS, scale=1.0), R=[smallB], W=[smallB])
                V(lambda e: e.reciprocal(rs, rs), R=[smallB], W=[smallB])
                V(lambda e: e.tensor_scalar(y32, y32, mv[:, 0:1], rs, ALU.subtract, ALU.mult),
                  R=[y32B, smallB], W=[y32B])
                V(lambda e, c=c: e.tensor_tensor(yg, y32, gT[:, c, :], ALU.mult), R=[y32B, gTB[c]], W=[ygB])
                for jj in range(4):
                    tr(tpb[:, jj * 128:(jj + 1) * 128], tpA_B, yg[:, jj * 128:(jj + 1) * 128], R=[ygB])
                for jj in range(4):
                    ch = h * 4 + jj
                    V(lambda e, jj=jj, ch=ch, tok=tok: e.tensor_scalar(
                        big[:, ch, tok], tpb[:, jj * 128:(jj + 1) * 128], gn[:, j * 32 + ch:j * 32 + ch + 1], None, ALU.mult),
                      R=[tpA_B, constB], W=[bigB[ch]])
                for dc in range(2):
                    tr(tp2[:, dc * 128:(dc + 1) * 128], a1B, kT[:, dc, tok], R=[kTB])
                V(lambda e, h=h: e.tensor_scalar(ktok, tp2[:, 0:256], dk_c[:, h:h + 1], None, ALU.mult),
                  R=[a1B, constB], W=[ktokB])
                for dc in range(2):
                    kvP, kvB = acc()
                    mm(kvP[:, :], kvB, [(ktok[:, dc * 128:(dc + 1) * 128], vT[:, c, :])], R=[ktokB, vTB[c]])
                    V(lambda e, kvP=kvP, dc=dc, Sh=Sh, h=h: e.scalar_tensor_tensor(
                        Sh[:, dc, :], Sh[:, dc, :], gC[h], kvP[:, :], ALU.mult, ALU.add),
                      R=[ShB, kvB], W=[ShB])
                if c < 3:
                    A(lambda e, Sh=Sh: e.activation(Sbf[:, :, :], Sh[:, :, :], AF.Copy), R=[ShB], W=[SbfB])
            if g < ngroups - 1:
                P.dma(sp, ssc_d[h], Sh[:, :, :], R=[ShB], W=[ssc_bufs[h]], sembuf=ShB)
        w_out = ret_w_out_d[j]
        for jo in range(KC):
            slab, sB = get_slab(w_out, 0, 4096, [(jo * 128, 128)])
            oP, oB = acc()
            mm(oP[:, :], oB, [(slab[:, kc, :], big[:, kc, :]) for kc in range(32)], R=[sB] + bigB[0:32])
            evac_out(jo, oP, oB)

    ckvT = arena[:, 0:4 * S].rearrange("p (c t) -> p c t", c=4)
    kropeT = arena[:, 4 * S:5 * S]
    ckvB = [PB("m_ckv%d" % i) for i in range(NG)]
    kropeB = [PB("m_krope%d" % i) for i in range(NG)]

    def mla(l, g):
        j = l // 2
        o = [5 * S]

        def carve(n_bf16):
            a = arena[:, o[0]:o[0] + n_bf16]
            o[0] += n_bf16
            return a
        cqT = carve(4 * G).rearrange("p (c t) -> p c t", c=4)
        qnT = carve(G)
        qrT = carve(G)
        KhT = carve(S)
        Vh = carve(16 * 128).rearrange("p (k v) -> p k v", k=16)
        PT = carve(2 * G).rearrange("p (k t) -> p k t", k=2)
        rec = carve(2 * G).bitcast(F32)
        assert o[0] <= 40 * 512
        cqB, qnB, qrB, KhB, VhB, recB = PB("m_cq"), PB("m_qn"), PB("m_qr"), PB("m_Kh"), PB("m_Vh"), PB("m_rec")
        PTB = [PB("m_PT0"), PB("m_PT1")]
        if g == 0:
            for eng in (pe, act, dve, sp):
                P.barrier_wait(eng, [b for n, b in REG.items() if n.startswith("r_")])
        w_in = mla_w_in_d[j]
        scale = float((MN + MR) ** -0.5)
        for t in range(8):
            if t % 2 == 0:
                slab, sB = get_slab(w_in, 0, D, [(t * 128, 256)])
            off = (t % 2) * 128
            p0, b0 = acc()
            mm(p0[:, :], b0, [(slab[:, kc, off:off + 128], hT[:, kc, :]) for kc in range(KC)], R=[sB] + hTB)
            A(lambda e, t=t, p0=p0: e.activation(oT_ap(8 + t), p0[:, :], AF.Copy), R=[b0], W=oT_bufs(8 + t))
            if t < 4:
                sq_accum(p0[:, :], b0, t, t == 0, t == 3, ssP, ssB)
            else:
                sq_accum(p0[:, :], b0, t, t == 4, t == 7, a1P, a1B)
        stats_rstd(MQL, 0)
        stats_rstd(MKL, 1, a1P, a1B)
        for c in range(4):
            gi = (j * 2 + 0) * 4 + c
            V(lambda e, c=c, gi=gi: e.scalar_tensor_tensor(cqT[:, c, :], oT_ap(8 + c), gqkv[:, gi:gi + 1],
                                                           rstd[:, 0, :], ALU.mult, ALU.mult),
              R=oT_bufs(8 + c) + [rstdB[0], constB], W=[cqB])
            gi2 = (j * 2 + 1) * 4 + c
            V(lambda e, c=c, gi2=gi2: e.scalar_tensor_tensor(ckvT[:, c, g * G:(g + 1) * G], oT_ap(12 + c),
                                                             gqkv[:, gi2:gi2 + 1], rstd[:, 1, :], ALU.mult, ALU.mult),
              R=oT_bufs(12 + c) + [rstdB[1], constB], W=[ckvB[g]])
        slab, sB = get_slab(w_in, 0, D, [(1024, 128)])
        pA, bA = acc()
        mm(pA[0:64, :], bA, [(slab[:, kc, 0:64], hT[:, kc, :]) for kc in range(KC)], R=[sB] + hTB)
        pB, bB = acc()
        mm(pB[0:64, :], bB, [(slab[:, kc, 64:128], hT[:, kc, :]) for kc in range(KC)], R=[sB] + hTB)
        V(lambda e, pA=pA: e.tensor_tensor(t1[0:64, :], pA[0:64, :], cosT[0:64, :], ALU.mult), R=[bA, tabsB], W=[tmpB[0]])
        V(lambda e, pB=pB: e.tensor_tensor(t2[0:64, :], pB[0:64, :], sinT[0:64, :], ALU.mult), R=[bB, tabsB], W=[tmpB[1]])
        V(lambda e: e.tensor_tensor(kropeT[0:64, g * G:(g + 1) * G], t1[0:64, :], t2[0:64, :], ALU.add),
          R=tmpB, W=[kropeB[g]])
        nkt = 4 * (g + 1)
        for h in range(MH):
            slq, sqB_ = get_slab(mla_w_uq_d[j], 0, MQL, [(h * 256, 256)])
            p0, b0 = acc()
            mm(p0[:, :], b0, [(slq[:, kc, 0:128], cqT[:, kc, :]) for kc in range(4)], R=[sqB_, cqB])
            A(lambda e, p0=p0: e.activation(qnT, p0[:, :], AF.Copy), R=[b0], W=[qnB])
            pA, bA = acc()
            mm(pA[0:64, :], bA, [(slq[:, kc, 128:192], cqT[:, kc, :]) for kc in range(4)], R=[sqB_, cqB])
            pB, bB = acc()
            mm(pB[0:64, :], bB, [(slq[:, kc, 192:256], cqT[:, kc, :]) for kc in range(4)], R=[sqB_, cqB])
            V(lambda e, pA=pA: e.tensor_tensor(t1[0:64, :], pA[0:64, :], cosT[0:64, :], ALU.mult), R=[bA, tabsB], W=[tmpB[0]])
            V(lambda e, pB=pB: e.tensor_tensor(t2[0:64, :], pB[0:64, :], sinT[0:64, :], ALU.mult), R=[bB, tabsB], W=[tmpB[1]])
            V(lambda e: e.tensor_tensor(qrT[0:64, :], t1[0:64, :], t2[0:64, :], ALU.add), R=tmpB, W=[qrB])
            slkv, skvB = get_slab(mla_w_ukv_d[j], 0, MKL, [(h * 256, 256)])
            for kg in range(g + 1):
                p0, b0 = acc()
                mm(p0[:, :], b0, [(slkv[:, kc, 0:128], ckvT[:, kc, kg * G:(kg + 1) * G]) for kc in range(4)],
                   R=[skvB, ckvB[kg]])
                A(lambda e, p0=p0, kg=kg: e.activation(KhT[:, kg * G:(kg + 1) * G], p0[:, :], AF.Copy), R=[b0], W=[KhB])
                p1, b1 = acc()
                for kt in range(4):
                    ktg = kg * 4 + kt
                    mm(p1[:, kt * 128:(kt + 1) * 128], b1,
                       [(ckvT[:, kc, ktg * 128:(ktg + 1) * 128], slkv[:, kc, 128:256]) for kc in range(4)],
                       R=[skvB, ckvB[kg]])
                V(lambda e, p1=p1, kg=kg: e.tensor_copy(
                    Vh[:, kg * 4:(kg + 1) * 4, :], p1[:, :].rearrange("p (k v) -> p k v", k=4)), R=[b1], W=[VhB])
            for kt in range(nkt):
                jd = kt - 4 * g
                kgi = kt // 4
                sP, sBk = acc()
                mm(sP[:, :], sBk, [(KhT[:, kt * 128:(kt + 1) * 128], qnT),
                                   (kropeT[0:64, kt * 128:(kt + 1) * 128], qrT[0:64, :])],
                   R=[KhB, qnB, qrB, kropeB[kgi]])
                k = kt % 2
                A(lambda e, sP=sP, k=k: e.activation(PT[:, k, :], sP[:, :], AF.Exp, scale=scale), R=[sBk], W=[PTB[k]])
                if jd >= 0:
                    V(lambda e, k=k, jd=jd: e.tensor_tensor(PT[:, k, :], PT[:, k, :], maskF[:, jd, :], ALU.mult),
                      R=[PTB[k], constB], W=[PTB[k]])
                mm(a1P[:, :], a1B, [(Vh[:, kt, :], PT[:, k, :])], R=[VhB, PTB[k]], start=(kt == 0), stop=(kt == nkt - 1))
                mm(a2P[:, :], a2B, [(ones_b, PT[:, k, :])], R=[constB, PTB[k]], start=(kt == 0), stop=(kt == nkt - 1))
            V(lambda e: e.reciprocal(rec, a2P[:, :]), R=[a2B], W=[recB])
            V(lambda e, h=h: e.tensor_tensor(big[:, h, :], a1P[:, :], rec, ALU.mult), R=[a1B, recB], W=[bigB[h]])
        w_out = mla_w_out_d[j]
        for jo2 in range(KC // 2):
            slab, sB = get_slab(w_out, 0, D, [(jo2 * 256, 256)])
            for sub in range(2):
                jo = jo2 * 2 + sub
                oP, oB = acc()
                mm(oP[:, :], oB, [(slab[:, kc, sub * 128:(sub + 1) * 128], big[:, kc, :]) for kc in range(KC)],
                   R=[sB] + bigB[0:KC])
                evac_out(jo, oP, oB)

    all_arena = [Buf("arena_guard")]
    for l in range(nlayers):
        is_mla = (l % 2 == 1)
        if dbg == 'allret':
            is_mla = False
        if dbg == 'allmla':
            is_mla = True
        for g in range(ngroups):
            src = xT_d if l == 0 else xs_d
            srcB = [] if l == 0 else [xs_bufs[g]]
            P.dma(sp, xg[:, :, :], src[:, :, g * G:(g + 1) * G].rearrange("c p t -> p c t"),
                  R=srcB, W=xgB)
            make_tables(g, is_mla)
            pre_norm(0, l)
            if is_mla:
                mla(l, g)
            else:
                retention(l, g)
            post_norm(1, l)
            pre_norm(2, l)
            ffn(l)
            post_norm(3, l)
            last = (l == nlayers - 1)
            dst = outT_d if last else xs_d
            dstB = out_bufs[g] if last else xs_bufs[g]
            P.dma(sp, dst[:, :, g * G:(g + 1) * G].rearrange("c p t -> p c t"), xg[:, :, :], R=xgB, W=[dstB], sembuf=xgB[0])
    P.barrier_wait(sp, out_bufs)

    sems = {}
    for e in P.engs:
        sems[e.name] = es.enter_context(nc.semaphore("s_" + e.name))
    for k in P.dma_sems:
        sems[k] = es.enter_context(nc.semaphore("s_" + k))
    block = es.enter_context(nc.Block())

    def replay(eng, handle):
        for it in eng.q:
            if it[0] == "wait":
                handle.wait_ge(sems[it[1]], it[2])
            elif it[0] == "op":
                ins = it[1](handle)
                ins.then_inc(sems[eng.name], 1)
            else:
                handle.dma_start(out=it[1], in_=it[2]).then_inc(sems[it[3]], 16)

    @block.tensor
    def _(e):
        replay(pe, e)

    @block.scalar
    def _(e):
        replay(act, e)

    @block.vector
    def _(e):
        replay(dve, e)

    @block.gpsimd
    def _(e):
        replay(pool, e)

    @block.sync
    def _(e):
        replay(sp, e)

    es.close()
    return nc


def _consts():
    f32 = np.float32
    hh = np.arange(RH, dtype=f32)
    log_gamma = np.log1p(-np.exp2(-5.0 - hh)).astype(f32)
    idx = np.arange(RC, dtype=f32)
    k = idx[:, None]
    q = idx[None, :]
    rel = q - k
    DinT = np.where(rel[None] >= 0, np.exp(log_gamma[:, None, None] * np.maximum(rel, 0)[None]), 0.0).astype(f32) / 16.0
    dq = np.exp(log_gamma[None, :] * (idx[:, None] + 1.0)).astype(f32)
    dk = (np.exp(log_gamma[None, :] * (RC - 1.0 - idx[:, None])) / 16.0).astype(f32)
    p = np.arange(128)
    inv_ret = (THETA ** (-(np.arange(0, 256, 2, dtype=f32)) / f32(256))).astype(f32)
    inv32 = (THETA ** (-(np.arange(0, 64, 2, dtype=f32)) / f32(64))).astype(f32)
    inv_mla = inv32[p % 32]
    sgn = np.where((p % 64) < 32, -1.0, 1.0).astype(f32)
    cf = np.zeros((128, 8 * 128 + 8 + 8 + 4 + 128), f32)
    cf[:, 0:1024] = DinT.transpose(1, 0, 2).reshape(128, 1024)
    cf[:, 1024:1032] = dq
    cf[:, 1032:1040] = dk
    cf[:, 1040] = inv_ret / f32(2 * np.pi)
    cf[:, 1041] = inv_mla / f32(2 * np.pi)
    cf[:, 1042] = sgn
    cf[:, 1044:1044 + 128] = 1.0
    cb = np.zeros((128, 128 + 128 + 2048), f32)
    cb[:, 0:128] = np.eye(128)
    cb[:, 128:256] = 1.0
    kk = np.arange(128)[:, None]
    qq = np.arange(512)[None, :]
    for jd in range(4):
        cb[:, 256 + jd * 512:256 + (jd + 1) * 512] = (qq >= jd * 128 + kk)
    return cf, cb.astype(ml_dtypes.bfloat16)


def _fm(v):
    return np.ascontiguousarray(v.reshape(-1, 128).T)


_NC_CACHE = {}


def kernel(x, positions, norm_mix_pre, norm_mix_post, norm_ffn_pre, norm_ffn_post,
           ret_w_in, ret_gn_g, ret_w_out,
           mla_w_in, mla_g_q, mla_g_kv, mla_w_uq, mla_w_ukv, mla_w_out,
           ffn_w_gu, ffn_w_down):
    f32 = np.float32
    x = np.asarray(x, f32)
    cf, cb = _consts()
    gains = np.concatenate([_fm(np.asarray(a, f32)[l]) for a in (norm_mix_pre, norm_mix_post, norm_ffn_pre, norm_ffn_post)
                            for l in range(L)], axis=1)
    gnp = np.concatenate([_fm(np.asarray(ret_gn_g, f32)[j]) for j in range(2)], axis=1)
    gqkv = np.concatenate([_fm(np.asarray(a, f32)[j]) for j in range(2) for a in (mla_g_q, mla_g_kv)], axis=1)
    mla_w_in = np.asarray(mla_w_in, f32)
    w_in_ext = np.concatenate([mla_w_in, mla_w_in[:, :, 1056:1088], mla_w_in[:, :, 1024:1056]], axis=2)
    wq = np.asarray(mla_w_uq, f32).reshape(2, MQL, MH, MN + MR)
    wq_ext = np.concatenate([wq, wq[..., MN + 32:MN + 64], wq[..., MN:MN + 32]], axis=3).reshape(2, MQL, MH * 256)
    shared = {
        "gains": np.ascontiguousarray(gains), "gn": np.ascontiguousarray(gnp), "gqkv": np.ascontiguousarray(gqkv),
        "cf": cf, "cb": cb,
        "ret_w_in": np.asarray(ret_w_in, f32), "ret_w_out": np.asarray(ret_w_out, f32),
        "mla_w_in": np.ascontiguousarray(w_in_ext), "mla_w_uq": np.ascontiguousarray(wq_ext),
        "mla_w_ukv": np.asarray(mla_w_ukv, f32), "mla_w_out": np.asarray(mla_w_out, f32),
        "ffn_w_gu": np.asarray(ffn_w_gu, f32), "ffn_w_down": np.asarray(ffn_w_down, f32),
    }
    in_maps = []
    for b in range(NB):
        m = dict(shared)
        m["xT"] = np.ascontiguousarray(x[b].T).reshape(KC, 128, S)
        m["pos"] = np.ascontiguousarray(np.asarray(positions)[b].astype(np.int32).reshape(1, S))
        in_maps.append(m)
    if "nc" not in _NC_CACHE:
        _NC_CACHE["nc"] = build()
    res = run_bass_kernel_spmd(_NC_CACHE["nc"], in_maps, core_ids=list(range(NB)))
    out = np.stack([np.asarray(res.results[b]["outT"]).reshape(D, S).T for b in range(NB)], axis=0)
    return np.ascontiguousarray(out.astype(f32))
```

```python
import numpy as np
import ml_dtypes
import concourse.bass as bass
import concourse.mybir as mybir
from concourse.bass_utils import run_bass_kernel_spmd

F32 = mybir.dt.float32
BF16 = mybir.dt.bfloat16
I32 = mybir.dt.int32
ALU = mybir.AluOpType
AF = mybir.ActivationFunctionType

D = 2048
S = 2048
NB = 4
TP = 2
NCORE = NB * TP
L = 4
G = 512
NG = S // G
KC = D // 128
DFF = 5632
NF = DFF // 128
NFL = NF // TP
RH, RDK, RDV, RC = 8, 256, 512, 128
RHL = RH // TP
MHL = 16 // TP
MH, MN, MR, MV, MQL, MKL = 16, 128, 64, 128, 512, 512
EPS = 1e-6
THETA = 10000.0
PI = float(np.pi)
SLOT = 4096
NSLOT = 3
SAME_ENGINE_SYNC = True


class Buf:
    __slots__ = ("name", "w", "r", "dsem")

    def __init__(self, name):
        self.name = name
        self.w = None
        self.r = {}
        self.dsem = None


class Eng:
    def __init__(self, name, is_pe=False):
        self.name = name
        self.q = []
        self.cnt = 0
        self.seen = {}
        self.is_pe = is_pe


class Prog:
    def __init__(self, nc):
        self.nc = nc
        self.pe = Eng("pe", True)
        self.act = Eng("act")
        self.dve = Eng("dve")
        self.pool = Eng("pool")
        self.sp = Eng("sp")
        self.engs = [self.pe, self.act, self.dve, self.pool, self.sp]
        self.dma_sems = {}
        self.nd = 0

    def _deps(self, eng, R, W):
        deps = {}

        def add(d):
            if d is None:
                return
            k, v = d
            if deps.get(k, 0) < v:
                deps[k] = v
        for b in R:
            add(b.w)
        for b in W:
            add(b.w)
            for k, v in b.r.items():
                add((k, v))
        for k, v in deps.items():
            if k == eng.name and (eng.is_pe or not SAME_ENGINE_SYNC):
                continue
            if eng.seen.get(k, 0) < v:
                eng.q.append(("wait", k, v))
                eng.seen[k] = v

    def op(self, eng, fn, R=(), W=()):
        self._deps(eng, R, W)
        eng.cnt += 1
        me = (eng.name, eng.cnt)
        eng.q.append(("op", fn))
        for b in R:
            if b.r.get(me[0], 0) < me[1]:
                b.r[me[0]] = me[1]
        for b in W:
            b.w = me
            b.r = {}

    def dma(self, eng, out_ap, in_ap, R=(), W=(), sembuf=None):
        self._deps(eng, R, W)
        sb = sembuf if sembuf is not None else (W[0] if W else R[0])
        if sb.dsem is None:
            sb.dsem = "d%d" % self.nd
            self.nd += 1
            self.dma_sems[sb.dsem] = 0
        self.dma_sems[sb.dsem] += 16
        me = (sb.dsem, self.dma_sems[sb.dsem])
        eng.q.append(("dma", out_ap, in_ap, sb.dsem))
        for b in R:
            if b.r.get(me[0], 0) < me[1]:
                b.r[me[0]] = me[1]
        for b in W:
            b.w = me
            b.r = {}

    def barrier_wait(self, eng, bufs):
        self._deps(eng, (), bufs)


def build(nlayers=L, ngroups=NG, dbg=None):
    nc = bass.Bass("TRN2", target_bir_lowering=False)
    P = Prog(nc)
    pe, act, dve, pool, sp = P.pe, P.act, P.dve, P.pool, P.sp

    def din(name, shape, dt=F32):
        return nc.dram_tensor(name, list(shape), dt, kind="ExternalInput")
    xT_d = din("xT", [KC, 128, S])
    pos_d = din("pos", [1, S], I32)
    gains_d = din("gains", [128, 16 * KC])
    gn_d = din("gn", [128, 32])
    gqkv_d = din("gqkv", [128, 16])
    cf_d = din("cf", [128, 8 * 128 + 8 + 8 + 4 + 128])
    cb_d = din("cb", [128, 128 + 128 + 4 * 512], BF16)
    ret_w_in_d = din("ret_w_in", [2, D, 6144])
    ret_w_out_d = din("ret_w_out", [2, 2048, D])
    mla_w_in_d = din("mla_w_in", [2, D, 1152])
    mla_w_uq_d = din("mla_w_uq", [2, MQL, 2048])
    mla_w_ukv_d = din("mla_w_ukv", [2, MKL, 2048])
    mla_w_out_d = din("mla_w_out", [2, 1024, D])
    ffn_w_gu_d = din("ffn_w_gu", [L, D, DFF])
    ffn_w_down_d = din("ffn_w_down", [L, DFF // TP, D])
    outT_d = nc.dram_tensor("outT", [KC, 128, S], F32, kind="ExternalOutput")
    xs_d = nc.dram_tensor("xs", [KC, 128, S], F32)
    ssc_d = nc.dram_tensor("ssc", [RHL, 128, 2, RDV], F32)
    arin_d = nc.dram_tensor("arin", [KC * 128, G], F32)
    arout_d = nc.dram_tensor("arout", [KC * 128, G], F32)
    arinB, aroutB = Buf("arin"), Buf("arout")
    xs_bufs = [Buf("xs%d" % g) for g in range(NG)]
    ssc_bufs = [Buf("ssc%d" % h) for h in range(RH)]
    out_bufs = [Buf("out%d" % g) for g in range(NG)]

    from contextlib import ExitStack
    es = ExitStack()

    def sb(name, shape, dt):
        return es.enter_context(nc.sbuf_tensor("sb_" + name, list(shape), dt))

    def ps(name, shape, dt):
        return es.enter_context(nc.psum_tensor(name, list(shape), dt))

    xg = sb("xg", [128, KC, G], F32)
    hoT = sb("hoT", [128, 2 * KC * G], BF16)
    big = sb("big", [128, NFL, G], BF16)
    ring = sb("ring", [128, NSLOT, SLOT], BF16)
    arena = sb("arena", [128, 40 * 512], BF16)
    sq = sb("sq", [128, 2, G], F32)
    rstd = sb("rstd", [128, 2, G], F32)
    tabs = sb("tabs", [128, 4, G], F32)
    posi = sb("posi", [128, G], I32)
    posf = sb("posf", [128, G], F32)
    gains = sb("gains", [128, 16 * KC], F32)
    gn = sb("gn", [128, 32], F32)
    gqkv = sb("gqkv", [128, 16], F32)
    cf = sb("cf", [128, 8 * 128 + 8 + 8 + 4 + 128], F32)
    cb = sb("cb", [128, 128 + 128 + 4 * 512], BF16)
    small = sb("small", [128, 16], F32)

    hT = hoT[:, 0:KC * G].rearrange("p (c t) -> p c t", c=KC)
    oT = hoT.bitcast(F32).rearrange("p (c t) -> p c t", c=KC) if hasattr(hoT, "bitcast") else None
    DinT = cf[:, 0:512].rearrange("p (h q) -> p h q", h=RHL)
    gC_c = cf[:, 512:516]
    dq_c = cf[:, 1024:1032]
    dk_c = cf[:, 1032:1040]
    inv_ret = cf[:, 1040:1041]
    inv_mla = cf[:, 1041:1042]
    sgn_mla = cf[:, 1042:1043]
    ones_f = cf[:, 1044:1044 + 128]
    ident_b = cb[:, 0:128]
    ones_b = cb[:, 128:256]
    maskF = cb[:, 256:256 + 2048].rearrange("p (j q) -> p j q", j=4)

    banks = [ps("pb%d" % i, [128, 512], F32) for i in range(7)]
    tpb = ps("tpb", [128, 1024], BF16)
    bankB = [Buf("bank%d" % i) for i in range(7)]
    tpA_B, tpB_B = Buf("tpA"), Buf("tpB")
    acc_i = [0]

    def acc():
        i = acc_i[0] % 4
        acc_i[0] += 1
        return banks[i], bankB[i]
    ssP, ssB = banks[4], bankB[4]
    a1P, a1B = banks[5], bankB[5]
    a2P, a2B = banks[6], bankB[6]
    tp2 = banks[5][:, :].bitcast(BF16)

    REG = {}

    def PB(name):
        if name not in REG:
            REG[name] = Buf(name)
        return REG[name]
    xgB = [Buf("xg%d" % c) for c in range(KC)]
    hTB = [Buf("hT%d" % c) for c in range(KC)]
    bigB = [Buf("big%d" % f) for f in range(NFL)]
    ringB = [Buf("ring%d" % i) for i in range(NSLOT)]
    sqB = [Buf("sq0"), Buf("sq1")]
    rstdB = [Buf("rstd0"), Buf("rstd1")]
    tabsB = Buf("tabs")
    posB = Buf("pos")
    constB = Buf("const")
    smallB = Buf("small")

    oTB_extra = [Buf("oTx%d" % j) for j in range(KC)]

    def oT_bufs(j):
        if j < 8:
            return [hTB[2 * j], hTB[2 * j + 1]]
        return [oTB_extra[j]]

    def oT_ap(j):
        return hoT[:, j * 1024:(j + 1) * 1024].bitcast(F32)

    ring_i = [0]

    def get_slab(dram2d, r0, nrows, col_ranges):
        i = ring_i[0] % NSLOT
        ring_i[0] += 1
        kc = nrows // 128
        ncols = sum(n for _, n in col_ranges)
        assert kc * ncols <= SLOT
        view = ring[:, i, 0:kc * ncols].rearrange("p (k n) -> p k n", k=kc)
        off = 0
        for (c0, n) in col_ranges:
            src = dram2d[r0:r0 + nrows, c0:c0 + n].rearrange("(k p) n -> p k n", p=128)
            P.dma(pool, view[:, :, off:off + n], src, W=[ringB[i]])
            off += n
        return view, ringB[i]

    def mm(out_ap, outB, pairs, R, start=True, stop=True):
        def fn(e, out_ap=out_ap, pairs=pairs, start=start, stop=stop):
            ins = None
            n = len(pairs)
            for i, (l, r) in enumerate(pairs):
                ins = e.matmul(out_ap, l, r, start=(start and i == 0), stop=(stop and i == n - 1))
            return ins
        P.op(pe, fn, R=R, W=[outB])

    def tr(out_ap, outB, in_ap, R):
        P.op(pe, lambda e: e.transpose(out_ap, in_ap, ident_b), R=list(R) + [constB], W=[outB])

    def A(fn, R, W):
        P.op(act, fn, R=R, W=W)

    def V(fn, R, W):
        P.op(dve, fn, R=R, W=W)

    for dst, src in ((gains, gains_d), (gn, gn_d), (gqkv, gqkv_d), (cf, cf_d), (cb, cb_d)):
        P.dma(sp, dst[:, :], src[:, :], W=[constB])

    def gain_ap(kind, l, c):
        i = (kind * 4 + l) * KC + c
        return gains[:, i:i + 1]

    def stats_rstd(n_feat, k, srcP=None, srcB=None):
        srcP = ssP if srcP is None else srcP
        srcB = ssB if srcB is None else srcB
        A(lambda e: e.activation(rstd[:, k, :], srcP[:, :], AF.Sqrt, bias=EPS, scale=1.0 / n_feat),
          R=[srcB], W=[rstdB[k]])
        V(lambda e: e.reciprocal(rstd[:, k, :], rstd[:, k, :]), R=[rstdB[k]], W=[rstdB[k]])

    def sq_accum(src_ap, srcB, idx, first, last, ssP_=None, ssB_=None):
        ssP_ = ssP if ssP_ is None else ssP_
        ssB_ = ssB if ssB_ is None else ssB_
        k = idx % 2
        A(lambda e: e.activation(sq[:, k, :], src_ap, AF.Square), R=[srcB], W=[sqB[k]])
        mm(ssP_[:, :], ssB_, [(ones_f, sq[:, k, :])], R=[sqB[k], constB], start=first, stop=last)

    def pre_norm(kind, l):
        for c in range(KC):
            sq_accum(xg[:, c, :], xgB[c], c, c == 0, c == KC - 1)
        stats_rstd(D, 0)
        for c in range(KC):
            V(lambda e, c=c: e.scalar_tensor_tensor(hT[:, c, :], xg[:, c, :], gain_ap(kind, l, c),
                                                    rstd[:, 0, :], ALU.mult, ALU.mult),
              R=[xgB[c], rstdB[0], constB], W=[hTB[c]])

    def evac_out(j, psP, psB):
        A(lambda e: e.activation(oT_ap(j), psP[:, :], AF.Copy), R=[psB], W=oT_bufs(j))

    all_oT = []
    for jj_ in range(KC):
        for b_ in oT_bufs(jj_):
            if b_ not in all_oT:
                all_oT.append(b_)

    def allreduce_out():
        oT_all = hoT[:, :].bitcast(F32).rearrange("p (c t) -> p c t", c=KC)
        P.dma(sp, arin_d[:, :].rearrange("(c p) t -> p c t", p=128), oT_all, R=all_oT, W=[arinB], sembuf=xgB[0])
        P.op(pool, lambda e: e.collective_compute("AllReduce", ALU.add, replica_groups=[[0, 1], [2, 3], [4, 5], [6, 7]],
                                                  ins=[arin_d.ap().opt()], outs=[arout_d.ap().opt()]),
             R=[arinB], W=[aroutB])
        P.dma(sp, oT_all, arout_d[:, :].rearrange("(c p) t -> p c t", p=128), R=[aroutB], W=all_oT, sembuf=xgB[0])
        for j in range(KC):
            sq_accum(oT_ap(j), oT_bufs(j)[0], j, j == 0, j == KC - 1)

    def post_norm(kind, l):
        allreduce_out()
        stats_rstd(D, 0)
        for c in range(KC):
            V(lambda e, c=c: e.scalar_tensor_tensor(oT_ap(c), oT_ap(c), gain_ap(kind, l, c),
                                                    rstd[:, 0, :], ALU.mult, ALU.mult),
              R=oT_bufs(c) + [rstdB[0], constB], W=oT_bufs(c))
            V(lambda e, c=c: e.tensor_tensor(xg[:, c, :], xg[:, c, :], oT_ap(c), ALU.add),
              R=oT_bufs(c) + [xgB[c]], W=[xgB[c]])

    def ffn(l):
        w2 = ffn_w_gu_d[l]
        for f in range(NFL):
            slab, sB = get_slab(w2, 0, D, [(f * 128, 128), (DFF // TP + f * 128, 128)])
            gP, gB = acc()
            mm(gP[:, :], gB, [(slab[:, kc, 0:128], hT[:, kc, :]) for kc in range(KC)], R=[sB] + hTB)
            uP, uB = acc()
            mm(uP[:, :], uB, [(slab[:, kc, 128:256], hT[:, kc, :]) for kc in range(KC)], R=[sB] + hTB)
            k = f % 2
            A(lambda e, gP=gP, k=k: e.activation(sq[:, k, :], gP[:, :], AF.Silu), R=[gB], W=[sqB[k]])
            V(lambda e, uP=uP, k=k, f=f: e.tensor_tensor(big[:, f, :], sq[:, k, :], uP[:, :], ALU.mult),
              R=[sqB[k], uB], W=[bigB[f]])
        wd = ffn_w_down_d[l]
        for j in range(KC):
            slA, bA = get_slab(wd, 0, 2816, [(j * 128, 128)])
            oP, oB = acc()
            pairs = [(slA[:, kc, :], big[:, kc, :]) for kc in range(NFL)]
            mm(oP[:, :], oB, pairs, R=[bA] + bigB)
            evac_out(j, oP, oB)

    cosT = tabs[:, 0, :]
    sinT = tabs[:, 1, :]
    t1 = tabs[:, 2, :]
    t2 = tabs[:, 3, :]
    tmpB = [Buf("t1"), Buf("t2")]

    def make_tables(g, mla):
        P.dma(sp, posi[:, :], pos_d[0:1, g * G:(g + 1) * G].partition_broadcast(128), W=[posB])
        V(lambda e: e.tensor_copy(posf[:, :], posi[:, :]), R=[posB], W=[posB])
        inv = inv_mla if mla else inv_ret
        for ti, shift in ((1, 0.0), (0, 0.25)):
            V(lambda e, ti=ti, shift=shift: e.tensor_scalar(tabs[:, ti, :], posf[:, :], inv, shift, ALU.mult, ALU.add),
              R=[posB, constB], W=[tabsB])
            V(lambda e, ti=ti: e.tensor_copy(posi[:, :], tabs[:, ti, :]), R=[tabsB], W=[posB])
            V(lambda e: e.tensor_copy(t1, posi[:, :]), R=[posB], W=[tmpB[0]])
            V(lambda e, ti=ti: e.tensor_tensor(tabs[:, ti, :], tabs[:, ti, :], t1, ALU.subtract), R=[tabsB, tmpB[0]], W=[tabsB])
            A(lambda e, ti=ti: e.activation(tabs[:, ti, :], tabs[:, ti, :], AF.Sin, scale=2 * PI), R=[tabsB], W=[tabsB])
        if mla:
            V(lambda e: e.tensor_scalar(tabs[:, 1, :], tabs[:, 1, :], sgn_mla, None, ALU.mult),
              R=[tabsB, constB], W=[tabsB])

    def retention(l, g):
        j = l // 2
        w_in = ret_w_in_d[j]
        o = [0]

        def carve(n_bf16):
            a = arena[:, o[0]:o[0] + n_bf16]
            o[0] += n_bf16
            return a
        qT = carve(2 * G).rearrange("p (c t) -> p c t", c=2)
        kT = carve(2 * G).rearrange("p (c t) -> p c t", c=2)
        vT = carve(4 * RDV).rearrange("p (c t) -> p c t", c=4)
        gT = carve(4 * RDV).rearrange("p (c t) -> p c t", c=4)
        ktok = carve(256)
        sTb = carve(128)
        Sbf = carve(2 * RDV).rearrange("p (c t) -> p c t", c=2)
        yg = carve(RDV)
        S32 = [carve(2 * 2 * RDV).bitcast(F32).rearrange("p (c t) -> p c t", c=2) for _ in range(2)]
        y32 = carve(2 * RDV).bitcast(F32)
        i32 = carve(2 * RDV).bitcast(F32)
        st6 = small[:, 0:6]
        mv = small[:, 6:8]
        rs = small[:, 8:9]
        qTB, kTB = PB("r_qT"), PB("r_kT")
        vTB = [PB("r_v%d" % i) for i in range(4)]
        gTB = [PB("r_g%d" % i) for i in range(4)]
        ktokB, sTbB, SbfB, ygB, y32B, i32B = PB("r_ktok"), PB("r_sTb"), PB("r_Sbf"), PB("r_yg"), PB("r_y32"), PB("r_i32")
        S32B = [PB("r_S32a"), PB("r_S32b")]
        if g == 0:
            for eng in (pe, act, dve, sp):
                P.barrier_wait(eng, [b for n, b in REG.items() if n.startswith("m_")])
        gC = [float(np.exp(np.float32(np.log1p(-np.exp2(np.float32(-5.0 - h)))) * np.float32(RC))) for h in range(RH)]

        for h in range(RHL):
            Sh, ShB = S32[h % 2], S32B[h % 2]
            if g == 0:
                V(lambda e, Sh=Sh: e.memset(Sh[:, :, :], 0.0), R=[], W=[ShB])
            else:
                P.dma(sp, Sh[:, :, :], ssc_d[h], R=[ssc_bufs[h]], W=[ShB])
            A(lambda e, Sh=Sh: e.activation(Sbf[:, :, :], Sh[:, :, :], AF.Copy), R=[ShB], W=[SbfB])

            for which, base, dst, dstB in ((0, h * 256, qT, qTB), (1, 1024 + h * 256, kT, kTB)):
                slab, sB = get_slab(w_in, 0, D, [(base, 256)])
                p0, b0 = acc()
                mm(p0[:, :], b0, [(slab[:, kc, 0:128], hT[:, kc, :]) for kc in range(KC)], R=[sB] + hTB)
                p1, b1 = acc()
                mm(p1[:, :], b1, [(slab[:, kc, 128:256], hT[:, kc, :]) for kc in range(KC)], R=[sB] + hTB)
                V(lambda e, p0=p0: e.tensor_tensor(t1, p0[:, :], cosT, ALU.mult), R=[b0, tabsB], W=[tmpB[0]])
                V(lambda e, p1=p1: e.tensor_tensor(t2, p1[:, :], sinT, ALU.mult), R=[b1, tabsB], W=[tmpB[1]])
                V(lambda e, dst=dst: e.tensor_tensor(dst[:, 0, :], t1, t2, ALU.subtract), R=tmpB, W=[dstB])
                V(lambda e, p0=p0: e.tensor_tensor(t1, p0[:, :], sinT, ALU.mult), R=[b0, tabsB], W=[tmpB[0]])
                V(lambda e, p1=p1: e.tensor_tensor(t2, p1[:, :], cosT, ALU.mult), R=[b1, tabsB], W=[tmpB[1]])
                V(lambda e, dst=dst: e.tensor_tensor(dst[:, 1, :], t1, t2, ALU.add), R=tmpB, W=[dstB])
            for which, base, dst, dstBs in ((0, 2048 + h * 512, vT, vTB), (1, 4096 + h * 512, gT, gTB)):
                for half in range(2):
                    slab, sB = get_slab(w_in, 0, D, [(base + half * 256, 256)])
                    for tt in range(4):
                        p0, b0 = acc()
                        mm(p0[:, 0:256], b0, [(hT[:, kc, tt * 128:(tt + 1) * 128], slab[:, kc, :]) for kc in range(KC)],
                           R=[sB] + hTB)
                        fn = AF.Copy if which == 0 else AF.Silu
                        A(lambda e, p0=p0, dst=dst, tt=tt, half=half, fn=fn:
                          e.activation(dst[:, tt, half * 256:(half + 1) * 256], p0[:, 0:256], fn),
                          R=[b0], W=[dstBs[tt]])
            for c in range(4):
                tok = slice(c * 128, (c + 1) * 128)
                sP, sBk = acc()
                mm(sP[:, 0:128], sBk, [(kT[:, dc, tok], qT[:, dc, tok]) for dc in range(2)], R=[kTB, qTB])
                V(lambda e, sP=sP, h=h: e.tensor_tensor(sTb, sP[:, 0:128], DinT[:, h, :], ALU.mult),
                  R=[sBk, constB], W=[sTbB])
                iaP, iaB = acc()
                mm(iaP[:, :], iaB, [(sTb, vT[:, c, :])], R=[sTbB, vTB[c]])
                ieP, ieB = acc()
                mm(ieP[:, :], ieB, [(qT[:, dc, tok], Sbf[:, dc, :]) for dc in range(2)], R=[qTB, SbfB])
                A(lambda e, ieP=ieP, h=h: e.activation(i32, ieP[:, :], AF.Copy, scale=dq_c[:, h:h + 1]),
                  R=[ieB, constB], W=[i32B])
                V(lambda e, iaP=iaP: e.tensor_tensor(y32, i32, iaP[:, :], ALU.add), R=[i32B, iaB], W=[y32B])
                V(lambda e: e.bn_stats(st6, y32), R=[y32B], W=[smallB])
                V(lambda e: e.bn_aggr(mv, st6), R=[smallB], W=[smallB])
                A(lambda e: e.activation(rs, mv[:, 1:2], AF.Sqrt, bias=EPS, scale=1.0), R=[smallB], W=[smallB])
                V(lambda e: e.reciprocal(rs, rs), R=[smallB], W=[smallB])
                V(lambda e: e.tensor_scalar(y32, y32, mv[:, 0:1], rs, ALU.subtract, ALU.mult),
                  R=[y32B, smallB], W=[y32B])
                V(lambda e, c=c: e.tensor_tensor(yg, y32, gT[:, c, :], ALU.mult), R=[y32B, gTB[c]], W=[ygB])
                for jj in range(4):
                    tr(tpb[:, jj * 128:(jj + 1) * 128], tpA_B, yg[:, jj * 128:(jj + 1) * 128], R=[ygB])
                for jj in range(4):
                    ch = h * 4 + jj
                    V(lambda e, jj=jj, ch=ch, tok=tok: e.tensor_scalar(
                        big[:, ch, tok], tpb[:, jj * 128:(jj + 1) * 128], gn[:, j * 16 + ch:j * 16 + ch + 1], None, ALU.mult),
                      R=[tpA_B, constB], W=[bigB[ch]])
                for dc in range(2):
                    tr(tp2[:, dc * 128:(dc + 1) * 128], a1B, kT[:, dc, tok], R=[kTB])
                V(lambda e, h=h: e.tensor_scalar(ktok, tp2[:, 0:256], dk_c[:, h:h + 1], None, ALU.mult),
                  R=[a1B, constB], W=[ktokB])
                for dc in range(2):
                    kvP, kvB = acc()
                    mm(kvP[:, :], kvB, [(ktok[:, dc * 128:(dc + 1) * 128], vT[:, c, :])], R=[ktokB, vTB[c]])
                    V(lambda e, kvP=kvP, dc=dc, Sh=Sh, h=h: e.scalar_tensor_tensor(
                        Sh[:, dc, :], Sh[:, dc, :], gC_c[:, h:h + 1], kvP[:, :], ALU.mult, ALU.add),
                      R=[ShB, kvB, constB], W=[ShB])
                if c < 3:
                    A(lambda e, Sh=Sh: e.activation(Sbf[:, :, :], Sh[:, :, :], AF.Copy), R=[ShB], W=[SbfB])
            if g < ngroups - 1:
                P.dma(sp, ssc_d[h], Sh[:, :, :], R=[ShB], W=[ssc_bufs[h]], sembuf=ShB)
        w_out = ret_w_out_d[j]
        for jo2 in range(KC // 2):
            slab, sB = get_slab(w_out, 0, 2048, [(jo2 * 256, 256)])
            for sub in range(2):
                jo = jo2 * 2 + sub
                oP, oB = acc()
                mm(oP[:, :], oB, [(slab[:, kc, sub * 128:(sub + 1) * 128], big[:, kc, :]) for kc in range(16)],
                   R=[sB] + bigB[0:16])
                evac_out(jo, oP, oB)

    ckvT = arena[:, 0:4 * S].rearrange("p (c t) -> p c t", c=4)
    kropeT = arena[:, 4 * S:5 * S]
    ckvB = [PB("m_ckv%d" % i) for i in range(NG)]
    kropeB = [PB("m_krope%d" % i) for i in range(NG)]

    def mla(l, g):
        j = l // 2
        o = [5 * S]

        def carve(n_bf16):
            a = arena[:, o[0]:o[0] + n_bf16]
            o[0] += n_bf16
            return a
        cqT = carve(4 * G).rearrange("p (c t) -> p c t", c=4)
        qnT = carve(G)
        qrT = carve(G)
        KhT = carve(S)
        Vh = carve(16 * 128).rearrange("p (k v) -> p k v", k=16)
        PT = carve(2 * G).rearrange("p (k t) -> p k t", k=2)
        rec = carve(2 * G).bitcast(F32)
        assert o[0] <= 40 * 512
        cqB, qnB, qrB, KhB, VhB, recB = PB("m_cq"), PB("m_qn"), PB("m_qr"), PB("m_Kh"), PB("m_Vh"), PB("m_rec")
        PTB = [PB("m_PT0"), PB("m_PT1")]
        if g == 0:
            for eng in (pe, act, dve, sp):
                P.barrier_wait(eng, [b for n, b in REG.items() if n.startswith("r_")])
        w_in = mla_w_in_d[j]
        scale = float((MN + MR) ** -0.5)
        for t in range(8):
            if t % 2 == 0:
                slab, sB = get_slab(w_in, 0, D, [(t * 128, 256)])
            off = (t % 2) * 128
            p0, b0 = acc()
            mm(p0[:, :], b0, [(slab[:, kc, off:off + 128], hT[:, kc, :]) for kc in range(KC)], R=[sB] + hTB)
            A(lambda e, t=t, p0=p0: e.activation(oT_ap(8 + t), p0[:, :], AF.Copy), R=[b0], W=oT_bufs(8 + t))
            if t < 4:
                sq_accum(p0[:, :], b0, t, t == 0, t == 3, ssP, ssB)
            else:
                sq_accum(p0[:, :], b0, t, t == 4, t == 7, a1P, a1B)
        stats_rstd(MQL, 0)
        stats_rstd(MKL, 1, a1P, a1B)
        for c in range(4):
            gi = (j * 2 + 0) * 4 + c
            V(lambda e, c=c, gi=gi: e.scalar_tensor_tensor(cqT[:, c, :], oT_ap(8 + c), gqkv[:, gi:gi + 1],
                                                           rstd[:, 0, :], ALU.mult, ALU.mult),
              R=oT_bufs(8 + c) + [rstdB[0], constB], W=[cqB])
            gi2 = (j * 2 + 1) * 4 + c
            V(lambda e, c=c, gi2=gi2: e.scalar_tensor_tensor(ckvT[:, c, g * G:(g + 1) * G], oT_ap(12 + c),
                                                             gqkv[:, gi2:gi2 + 1], rstd[:, 1, :], ALU.mult, ALU.mult),
              R=oT_bufs(12 + c) + [rstdB[1], constB], W=[ckvB[g]])
        slab, sB = get_slab(w_in, 0, D, [(1024, 128)])
        pA, bA = acc()
        mm(pA[0:64, :], bA, [(slab[:, kc, 0:64], hT[:, kc, :]) for kc in range(KC)], R=[sB] + hTB)
        pB, bB = acc()
        mm(pB[0:64, :], bB, [(slab[:, kc, 64:128], hT[:, kc, :]) for kc in range(KC)], R=[sB] + hTB)
        V(lambda e, pA=pA: e.tensor_tensor(t1[0:64, :], pA[0:64, :], cosT[0:64, :], ALU.mult), R=[bA, tabsB], W=[tmpB[0]])
        V(lambda e, pB=pB: e.tensor_tensor(t2[0:64, :], pB[0:64, :], sinT[0:64, :], ALU.mult), R=[bB, tabsB], W=[tmpB[1]])
        V(lambda e: e.tensor_tensor(kropeT[0:64, g * G:(g + 1) * G], t1[0:64, :], t2[0:64, :], ALU.add),
          R=tmpB, W=[kropeB[g]])
        nkt = 4 * (g + 1)
        for h in range(MHL):
            slq, sqB_ = get_slab(mla_w_uq_d[j], 0, MQL, [(h * 256, 256)])
            p0, b0 = acc()
            mm(p0[:, :], b0, [(slq[:, kc, 0:128], cqT[:, kc, :]) for kc in range(4)], R=[sqB_, cqB])
            A(lambda e, p0=p0: e.activation(qnT, p0[:, :], AF.Copy), R=[b0], W=[qnB])
            pA, bA = acc()
            mm(pA[0:64, :], bA, [(slq[:, kc, 128:192], cqT[:, kc, :]) for kc in range(4)], R=[sqB_, cqB])
            pB, bB = acc()
            mm(pB[0:64, :], bB, [(slq[:, kc, 192:256], cqT[:, kc, :]) for kc in range(4)], R=[sqB_, cqB])
            V(lambda e, pA=pA: e.tensor_tensor(t1[0:64, :], pA[0:64, :], cosT[0:64, :], ALU.mult), R=[bA, tabsB], W=[tmpB[0]])
            V(lambda e, pB=pB: e.tensor_tensor(t2[0:64, :], pB[0:64, :], sinT[0:64, :], ALU.mult), R=[bB, tabsB], W=[tmpB[1]])
            V(lambda e: e.tensor_tensor(qrT[0:64, :], t1[0:64, :], t2[0:64, :], ALU.add), R=tmpB, W=[qrB])
            slkv, skvB = get_slab(mla_w_ukv_d[j], 0, MKL, [(h * 256, 256)])
            for kg in range(g + 1):
                p0, b0 = acc()
                mm(p0[:, :], b0, [(slkv[:, kc, 0:128], ckvT[:, kc, kg * G:(kg + 1) * G]) for kc in range(4)],
                   R=[skvB, ckvB[kg]])
                A(lambda e, p0=p0, kg=kg: e.activation(KhT[:, kg * G:(kg + 1) * G], p0[:, :], AF.Copy), R=[b0], W=[KhB])
                p1, b1 = acc()
                for kt in range(4):
                    ktg = kg * 4 + kt
                    mm(p1[:, kt * 128:(kt + 1) * 128], b1,
                       [(ckvT[:, kc, ktg * 128:(ktg + 1) * 128], slkv[:, kc, 128:256]) for kc in range(4)],
                       R=[skvB, ckvB[kg]])
                V(lambda e, p1=p1, kg=kg: e.tensor_copy(
                    Vh[:, kg * 4:(kg + 1) * 4, :], p1[:, :].rearrange("p (k v) -> p k v", k=4)), R=[b1], W=[VhB])
            for kt in range(nkt):
                jd = kt - 4 * g
                kgi = kt // 4
                sP, sBk = acc()
                mm(sP[:, :], sBk, [(KhT[:, kt * 128:(kt + 1) * 128], qnT),
                                   (kropeT[0:64, kt * 128:(kt + 1) * 128], qrT[0:64, :])],
                   R=[KhB, qnB, qrB, kropeB[kgi]])
                k = kt % 2
                A(lambda e, sP=sP, k=k: e.activation(PT[:, k, :], sP[:, :], AF.Exp, scale=scale), R=[sBk], W=[PTB[k]])
                if jd >= 0:
                    V(lambda e, k=k, jd=jd: e.tensor_tensor(PT[:, k, :], PT[:, k, :], maskF[:, jd, :], ALU.mult),
                      R=[PTB[k], constB], W=[PTB[k]])
                mm(a1P[:, :], a1B, [(Vh[:, kt, :], PT[:, k, :])], R=[VhB, PTB[k]], start=(kt == 0), stop=(kt == nkt - 1))
                mm(a2P[:, :], a2B, [(ones_b, PT[:, k, :])], R=[constB, PTB[k]], start=(kt == 0), stop=(kt == nkt - 1))
            V(lambda e: e.reciprocal(rec, a2P[:, :]), R=[a2B], W=[recB])
            V(lambda e, h=h: e.tensor_tensor(big[:, h, :], a1P[:, :], rec, ALU.mult), R=[a1B, recB], W=[bigB[h]])
        w_out = mla_w_out_d[j]
        for jo2 in range(KC // 2):
            slab, sB = get_slab(w_out, 0, 1024, [(jo2 * 256, 256)])
            for sub in range(2):
                jo = jo2 * 2 + sub
                oP, oB = acc()
                mm(oP[:, :], oB, [(slab[:, kc, sub * 128:(sub + 1) * 128], big[:, kc, :]) for kc in range(MHL)],
                   R=[sB] + bigB[0:MHL])
                evac_out(jo, oP, oB)

    all_arena = [Buf("arena_guard")]
    for l in range(nlayers):
        is_mla = (l % 2 == 1)
        if dbg == 'allret':
            is_mla = False
        if dbg == 'allmla':
            is_mla = True
        for g in range(ngroups):
            src = xT_d if l == 0 else xs_d
            srcB = [] if l == 0 else [xs_bufs[g]]
            P.dma(sp, xg[:, :, :], src[:, :, g * G:(g + 1) * G].rearrange("c p t -> p c t"),
                  R=srcB, W=xgB)
            make_tables(g, is_mla)
            pre_norm(0, l)
            if is_mla:
                mla(l, g)
            else:
                retention(l, g)
            post_norm(1, l)
            pre_norm(2, l)
            ffn(l)
            post_norm(3, l)
            last = (l == nlayers - 1)
            dst = outT_d if last else xs_d
            dstB = out_bufs[g] if last else xs_bufs[g]
            P.dma(sp, dst[:, :, g * G:(g + 1) * G].rearrange("c p t -> p c t"), xg[:, :, :], R=xgB, W=[dstB], sembuf=xgB[0])
    P.barrier_wait(sp, out_bufs)

    sems = {}
    for e in P.engs:
        sems[e.name] = es.enter_context(nc.semaphore("s_" + e.name))
    for k in P.dma_sems:
        sems[k] = es.enter_context(nc.semaphore("s_" + k))
    block = es.enter_context(nc.Block())

    def replay(eng, handle):
        for it in eng.q:
            if it[0] == "wait":
                handle.wait_ge(sems[it[1]], it[2])
            elif it[0] == "op":
                ins = it[1](handle)
                ins.then_inc(sems[eng.name], 1)
            else:
                handle.dma_start(out=it[1], in_=it[2]).then_inc(sems[it[3]], 16)

    @block.tensor
    def _(e):
        replay(pe, e)

    @block.scalar
    def _(e):
        replay(act, e)

    @block.vector
    def _(e):
        replay(dve, e)

    @block.gpsimd
    def _(e):
        replay(pool, e)

    @block.sync
    def _(e):
        replay(sp, e)

    es.close()
    return nc


def _consts(par):
    f32 = np.float32
    hh = (np.arange(RHL) + par * RHL).astype(f32)
    log_gamma = np.log1p(-np.exp2(-5.0 - hh)).astype(f32)
    idx = np.arange(RC, dtype=f32)
    k = idx[:, None]
    q = idx[None, :]
    rel = q - k
    DinT = np.where(rel[None] >= 0, np.exp(log_gamma[:, None, None] * np.maximum(rel, 0)[None]), 0.0).astype(f32) / 16.0
    dq = np.exp(log_gamma[None, :] * (idx[:, None] + 1.0)).astype(f32)
    dk = (np.exp(log_gamma[None, :] * (RC - 1.0 - idx[:, None])) / 16.0).astype(f32)
    gC = np.exp(log_gamma * f32(RC)).astype(f32)
    p = np.arange(128)
    inv_ret = (THETA ** (-(np.arange(0, 256, 2, dtype=f32)) / f32(256))).astype(f32)
    inv32 = (THETA ** (-(np.arange(0, 64, 2, dtype=f32)) / f32(64))).astype(f32)
    inv_mla = inv32[p % 32]
    sgn = np.where((p % 64) < 32, -1.0, 1.0).astype(f32)
    cf = np.zeros((128, 8 * 128 + 8 + 8 + 4 + 128), f32)
    cf[:, 0:512] = DinT.transpose(1, 0, 2).reshape(128, 512)
    cf[:, 512:516] = gC[None, :]
    cf[:, 1024:1028] = dq
    cf[:, 1032:1036] = dk
    cf[:, 1040] = inv_ret / f32(2 * np.pi)
    cf[:, 1041] = inv_mla / f32(2 * np.pi)
    cf[:, 1042] = sgn
    cf[:, 1044:1044 + 128] = 1.0
    cb = np.zeros((128, 128 + 128 + 2048), f32)
    cb[:, 0:128] = np.eye(128)
    cb[:, 128:256] = 1.0
    kk = np.arange(128)[:, None]
    qq = np.arange(512)[None, :]
    for jd in range(4):
        cb[:, 256 + jd * 512:256 + (jd + 1) * 512] = (qq >= jd * 128 + kk)
    return cf, cb.astype(ml_dtypes.bfloat16)


def _fm(v):
    return np.ascontiguousarray(v.reshape(-1, 128).T)


_NC_CACHE = {}


def kernel(x, positions, norm_mix_pre, norm_mix_post, norm_ffn_pre, norm_ffn_post,
           ret_w_in, ret_gn_g, ret_w_out,
           mla_w_in, mla_g_q, mla_g_kv, mla_w_uq, mla_w_ukv, mla_w_out,
           ffn_w_gu, ffn_w_down):
    f32 = np.float32
    x = np.asarray(x, f32)
    c = np.ascontiguousarray
    gains = np.concatenate([_fm(np.asarray(a, f32)[l]) for a in (norm_mix_pre, norm_mix_post, norm_ffn_pre, norm_ffn_post)
                            for l in range(L)], axis=1)
    gqkv = np.concatenate([_fm(np.asarray(a, f32)[j]) for j in range(2) for a in (mla_g_q, mla_g_kv)], axis=1)
    mla_w_in = np.asarray(mla_w_in, f32)
    w_in_ext = c(np.concatenate([mla_w_in, mla_w_in[:, :, 1056:1088], mla_w_in[:, :, 1024:1056]], axis=2))
    wq = np.asarray(mla_w_uq, f32).reshape(2, MQL, MH, MN + MR)
    wq_ext = np.concatenate([wq, wq[..., MN + 32:MN + 64], wq[..., MN:MN + 32]], axis=3)
    ret_w_in = np.asarray(ret_w_in, f32)
    ret_w_out = np.asarray(ret_w_out, f32)
    ret_gn_g = np.asarray(ret_gn_g, f32)
    mla_w_ukv = np.asarray(mla_w_ukv, f32).reshape(2, MKL, MH, 256)
    mla_w_out = np.asarray(mla_w_out, f32)
    ffn_w_gu = np.asarray(ffn_w_gu, f32)
    ffn_w_down = np.asarray(ffn_w_down, f32)
    per_par = []
    for par in range(TP):
        cf, cb = _consts(par)
        hs = slice(par * RHL, (par + 1) * RHL)
        q_ = ret_w_in[:, :, 0:2048].reshape(2, D, RH, 256)[:, :, hs].reshape(2, D, -1)
        k_ = ret_w_in[:, :, 2048:4096].reshape(2, D, RH, 256)[:, :, hs].reshape(2, D, -1)
        v_ = ret_w_in[:, :, 4096:8192].reshape(2, D, RH, 512)[:, :, hs].reshape(2, D, -1)
        g_ = ret_w_in[:, :, 8192:12288].reshape(2, D, RH, 512)[:, :, hs].reshape(2, D, -1)
        ms = slice(par * MHL, (par + 1) * MHL)
        hid = slice(par * (DFF // TP), (par + 1) * (DFF // TP))
        per_par.append({
            "cf": cf, "cb": cb,
            "gn": c(np.concatenate([_fm(ret_gn_g[j][par * 2048:(par + 1) * 2048]) for j in range(2)], axis=1)),
            "ret_w_in": c(np.concatenate([q_, k_, v_, g_], axis=2)),
            "ret_w_out": c(ret_w_out[:, par * 2048:(par + 1) * 2048, :]),
            "mla_w_uq": c(wq_ext[:, :, ms].reshape(2, MQL, -1)),
            "mla_w_ukv": c(mla_w_ukv[:, :, ms].reshape(2, MKL, -1)),
            "mla_w_out": c(mla_w_out[:, par * 1024:(par + 1) * 1024, :]),
            "ffn_w_gu": c(np.concatenate([ffn_w_gu[:, :, 0:DFF][:, :, hid], ffn_w_gu[:, :, DFF:][:, :, hid]], axis=2)),
            "ffn_w_down": c(ffn_w_down[:, hid, :]),
        })
    shared = {"gains": c(gains), "gqkv": c(gqkv), "mla_w_in": w_in_ext}
    in_maps = []
    for core in range(NCORE):
        b, par = core // TP, core % TP
        m = dict(shared)
        m.update(per_par[par])
        m["xT"] = c(x[b].T).reshape(KC, 128, S)
        m["pos"] = c(np.asarray(positions)[b].astype(np.int32).reshape(1, S))
        in_maps.append(m)
    if "nc" not in _NC_CACHE:
        _NC_CACHE["nc"] = build()
    res = run_bass_kernel_spmd(_NC_CACHE["nc"], in_maps, core_ids=list(range(NCORE)))
    out = np.stack([np.asarray(res.results[b * TP]["outT"]).reshape(D, S).T for b in range(NB)], axis=0)
    return np.ascontiguousarray(out.astype(f32))
```

```python
import numpy as np
import ml_dtypes
import concourse.bass as bass
import concourse.mybir as mybir
from concourse.bass_utils import run_bass_kernel_spmd

F32 = mybir.dt.float32
BF16 = mybir.dt.bfloat16
I32 = mybir.dt.int32
ALU = mybir.AluOpType
AF = mybir.ActivationFunctionType

D = 2048
S = 2048
NB = 4
TP = 2
NCORE = NB * TP
L = 4
G = 512
NG = S // G
KC = D // 128
DFF = 5632
NF = DFF // 128
NFL = NF // TP
RH, RDK, RDV, RC = 8, 256, 512, 128
RHL = RH // TP
MHL = 16 // TP
MH, MN, MR, MV, MQL, MKL = 16, 128, 64, 128, 512, 512
EPS = 1e-6
THETA = 10000.0
PI = float(np.pi)
SLOT = 4096
NSLOT = 5
SAME_ENGINE_SYNC = True


class Buf:
    __slots__ = ("name", "w", "r", "dsem")

    def __init__(self, name):
        self.name = name
        self.w = None
        self.r = {}
        self.dsem = None


class Eng:
    def __init__(self, name, is_pe=False):
        self.name = name
        self.q = []
        self.cnt = 0
        self.seen = {}
        self.is_pe = is_pe


class Prog:
    def __init__(self, nc):
        self.nc = nc
        self.pe = Eng("pe", True)
        self.act = Eng("act")
        self.dve = Eng("dve")
        self.pool = Eng("pool")
        self.sp = Eng("sp")
        self.engs = [self.pe, self.act, self.dve, self.pool, self.sp]
        self.dma_sems = {}
        self.nd = 0

    def _deps(self, eng, R, W):
        deps = {}

        def add(d):
            if d is None:
                return
            k, v = d
            if deps.get(k, 0) < v:
                deps[k] = v
        for b in R:
            add(b.w)
        for b in W:
            add(b.w)
            for k, v in b.r.items():
                add((k, v))
        for k, v in deps.items():
            if k == eng.name and (eng.is_pe or not SAME_ENGINE_SYNC):
                continue
            if eng.seen.get(k, 0) < v:
                eng.q.append(("wait", k, v))
                eng.seen[k] = v

    def op(self, eng, fn, R=(), W=()):
        self._deps(eng, R, W)
        eng.cnt += 1
        me = (eng.name, eng.cnt)
        eng.q.append(("op", fn))
        for b in R:
            if b.r.get(me[0], 0) < me[1]:
                b.r[me[0]] = me[1]
        for b in W:
            b.w = me
            b.r = {}

    def dma(self, eng, out_ap, in_ap, R=(), W=(), sembuf=None):
        self._deps(eng, R, W)
        sb = sembuf if sembuf is not None else (W[0] if W else R[0])
        if sb.dsem is None:
            sb.dsem = "d%d" % self.nd
            self.nd += 1
            self.dma_sems[sb.dsem] = 0
        self.dma_sems[sb.dsem] += 16
        me = (sb.dsem, self.dma_sems[sb.dsem])
        eng.q.append(("dma", out_ap, in_ap, sb.dsem))
        for b in R:
            if b.r.get(me[0], 0) < me[1]:
                b.r[me[0]] = me[1]
        for b in W:
            b.w = me
            b.r = {}

    def barrier_wait(self, eng, bufs):
        self._deps(eng, (), bufs)


def build(nlayers=L, ngroups=NG, dbg=None):
    nc = bass.Bass("TRN2", target_bir_lowering=False)
    P = Prog(nc)
    pe, act, dve, pool, sp = P.pe, P.act, P.dve, P.pool, P.sp

    def din(name, shape, dt=F32):
        return nc.dram_tensor(name, list(shape), dt, kind="ExternalInput")
    xT_d = din("xT", [KC, 128, S])
    pos_d = din("pos", [1, S], I32)
    gains_d = din("gains", [128, 16 * KC])
    gn_d = din("gn", [128, 32])
    gqkv_d = din("gqkv", [128, 16])
    cf_d = din("cf", [128, 8 * 128 + 8 + 8 + 4 + 128])
    cb_d = din("cb", [128, 128 + 128 + 4 * 512], BF16)
    ret_w_in_d = din("ret_w_in", [2, D, 6144])
    ret_w_out_d = din("ret_w_out", [2, 2048, D])
    mla_w_in_d = din("mla_w_in", [2, D, 1152])
    mla_w_uq_d = din("mla_w_uq", [2, MQL, 2048])
    mla_w_ukv_d = din("mla_w_ukv", [2, MKL, 2048])
    mla_w_out_d = din("mla_w_out", [2, 1024, D])
    ffn_w_gu_d = din("ffn_w_gu", [L, D, DFF])
    ffn_w_down_d = din("ffn_w_down", [L, DFF // TP, D])
    outT_d = nc.dram_tensor("outT", [KC, 128, S], F32, kind="ExternalOutput")
    xs_d = nc.dram_tensor("xs", [KC, 128, S], F32)
    ssc_d = nc.dram_tensor("ssc", [RHL, 128, 2, RDV], F32)
    arin_d = [nc.dram_tensor("arin%d" % i, [8 * 128, G], F32) for i in range(2)]
    arout_d = [nc.dram_tensor("arout%d" % i, [8 * 128, G], F32) for i in range(2)]
    arinB = [Buf("arin0"), Buf("arin1")]
    aroutB = [Buf("arout0"), Buf("arout1")]
    xs_bufs = [Buf("xs%d" % g) for g in range(NG)]
    ssc_bufs = [Buf("ssc%d" % h) for h in range(RH)]
    out_bufs = [Buf("out%d" % g) for g in range(NG)]

    from contextlib import ExitStack
    es = ExitStack()

    def sb(name, shape, dt):
        return es.enter_context(nc.sbuf_tensor("sb_" + name, list(shape), dt))

    def ps(name, shape, dt):
        return es.enter_context(nc.psum_tensor(name, list(shape), dt))

    xg = sb("xg", [128, KC, G], F32)
    hoT = sb("hoT", [128, 2 * KC * G], BF16)
    big = sb("big", [128, NFL, G], BF16)
    ring = sb("ring", [128, NSLOT, SLOT], BF16)
    arena = sb("arena", [128, 40 * 512], BF16)
    sq = sb("sq", [128, 2, G], F32)
    rstd = sb("rstd", [128, 2, G], F32)
    tabs = sb("tabs", [128, 4, G], F32)
    posi = sb("posi", [128, G], I32)
    posf = sb("posf", [128, G], F32)
    gains = sb("gains", [128, 16 * KC], F32)
    gn = sb("gn", [128, 32], F32)
    gqkv = sb("gqkv", [128, 16], F32)
    cf = sb("cf", [128, 8 * 128 + 8 + 8 + 4 + 128], F32)
    cb = sb("cb", [128, 128 + 128 + 4 * 512], BF16)
    small = sb("small", [128, 16], F32)

    hT = hoT[:, 0:KC * G].rearrange("p (c t) -> p c t", c=KC)
    oT = hoT.bitcast(F32).rearrange("p (c t) -> p c t", c=KC) if hasattr(hoT, "bitcast") else None
    DinT = cf[:, 0:512].rearrange("p (h q) -> p h q", h=RHL)
    gC_c = cf[:, 512:516]
    dq_c = cf[:, 1024:1032]
    dk_c = cf[:, 1032:1040]
    inv_ret = cf[:, 1040:1041]
    inv_mla = cf[:, 1041:1042]
    sgn_mla = cf[:, 1042:1043]
    ones_f = cf[:, 1044:1044 + 128]
    ident_b = cb[:, 0:128]
    ones_b = cb[:, 128:256]
    maskF = cb[:, 256:256 + 2048].rearrange("p (j q) -> p j q", j=4)

    banks = [ps("pb%d" % i, [128, 512], F32) for i in range(7)]
    tpb = ps("tpb", [128, 1024], BF16)
    bankB = [Buf("bank%d" % i) for i in range(7)]
    tpA_B, tpB_B = Buf("tpA"), Buf("tpB")
    acc_i = [0]

    def acc():
        i = acc_i[0] % 4
        acc_i[0] += 1
        return banks[i], bankB[i]
    ssP, ssB = banks[4], bankB[4]
    a1P, a1B = banks[5], bankB[5]
    a2P, a2B = banks[6], bankB[6]
    tp2 = banks[5][:, :].bitcast(BF16)

    REG = {}

    def PB(name):
        if name not in REG:
            REG[name] = Buf(name)
        return REG[name]
    xgB = [Buf("xg%d" % c) for c in range(KC)]
    hTB = [Buf("hT%d" % c) for c in range(KC)]
    bigB = [Buf("big%d" % f) for f in range(NFL)]
    ringB = [Buf("ring%d" % i) for i in range(NSLOT)]
    sqB = [Buf("sq0"), Buf("sq1")]
    rstdB = [Buf("rstd0"), Buf("rstd1")]
    tabsB = Buf("tabs")
    posB = Buf("pos")
    constB = Buf("const")
    smallB = Buf("small")

    oTB_extra = [Buf("oTx%d" % j) for j in range(KC)]

    def oT_bufs(j):
        if j < 8:
            return [hTB[2 * j], hTB[2 * j + 1]]
        return [oTB_extra[j]]

    def oT_ap(j):
        return hoT[:, j * 1024:(j + 1) * 1024].bitcast(F32)

    ring_i = [0]

    def get_slab(dram2d, r0, nrows, col_ranges):
        i = ring_i[0] % NSLOT
        ring_i[0] += 1
        kc = nrows // 128
        ncols = sum(n for _, n in col_ranges)
        assert kc * ncols <= SLOT
        view = ring[:, i, 0:kc * ncols].rearrange("p (k n) -> p k n", k=kc)
        off = 0
        for (c0, n) in col_ranges:
            src = dram2d[r0:r0 + nrows, c0:c0 + n].rearrange("(k p) n -> p k n", p=128)
            P.dma(pool, view[:, :, off:off + n], src, W=[ringB[i]])
            off += n
        return view, ringB[i]

    def mm(out_ap, outB, pairs, R, start=True, stop=True):
        def fn(e, out_ap=out_ap, pairs=pairs, start=start, stop=stop):
            ins = None
            n = len(pairs)
            for i, (l, r) in enumerate(pairs):
                ins = e.matmul(out_ap, l, r, start=(start and i == 0), stop=(stop and i == n - 1))
            return ins
        P.op(pe, fn, R=R, W=[outB])

    def tr(out_ap, outB, in_ap, R):
        P.op(pe, lambda e: e.transpose(out_ap, in_ap, ident_b), R=list(R) + [constB], W=[outB])

    def A(fn, R, W):
        P.op(act, fn, R=R, W=W)

    def V(fn, R, W):
        P.op(dve, fn, R=R, W=W)

    for dst, src in ((gains, gains_d), (gn, gn_d), (gqkv, gqkv_d), (cf, cf_d), (cb, cb_d)):
        P.dma(sp, dst[:, :], src[:, :], W=[constB])

    def gain_ap(kind, l, c):
        i = (kind * 4 + l) * KC + c
        return gains[:, i:i + 1]

    def stats_rstd(n_feat, k, srcP=None, srcB=None):
        srcP = ssP if srcP is None else srcP
        srcB = ssB if srcB is None else srcB
        A(lambda e: e.activation(rstd[:, k, :], srcP[:, :], AF.Sqrt, bias=EPS, scale=1.0 / n_feat),
          R=[srcB], W=[rstdB[k]])
        V(lambda e: e.reciprocal(rstd[:, k, :], rstd[:, k, :]), R=[rstdB[k]], W=[rstdB[k]])

    def sq_accum(src_ap, srcB, idx, first, last, ssP_=None, ssB_=None):
        ssP_ = ssP if ssP_ is None else ssP_
        ssB_ = ssB if ssB_ is None else ssB_
        k = idx % 2
        A(lambda e: e.activation(sq[:, k, :], src_ap, AF.Square), R=[srcB], W=[sqB[k]])
        mm(ssP_[:, :], ssB_, [(ones_f, sq[:, k, :])], R=[sqB[k], constB], start=first, stop=last)

    def pre_norm(kind, l):
        for c in range(KC):
            sq_accum(xg[:, c, :], xgB[c], c, c == 0, c == KC - 1)
        stats_rstd(D, 0)
        for c in range(KC):
            V(lambda e, c=c: e.scalar_tensor_tensor(hT[:, c, :], xg[:, c, :], gain_ap(kind, l, c),
                                                    rstd[:, 0, :], ALU.mult, ALU.mult),
              R=[xgB[c], rstdB[0], constB], W=[hTB[c]])

    def ar_half(hf):
        bufs = []
        for jj_ in range(hf * 8, hf * 8 + 8):
            for b_ in oT_bufs(jj_):
                if b_ not in bufs:
                    bufs.append(b_)
        view = hoT[:, hf * 8192:(hf + 1) * 8192].bitcast(F32).rearrange("p (c t) -> p c t", c=8)
        P.dma(sp, arin_d[hf][:, :].rearrange("(c p) t -> p c t", p=128), view, R=bufs, W=[arinB[hf]])
        P.op(pool, lambda e: e.collective_compute("AllReduce", ALU.add, replica_groups=[[0, 1], [2, 3], [4, 5], [6, 7]],
                                                  ins=[arin_d[hf].ap().opt()], outs=[arout_d[hf].ap().opt()]),
             R=[arinB[hf]], W=[aroutB[hf]])
        P.dma(sp, view, arout_d[hf][:, :].rearrange("(c p) t -> p c t", p=128), R=[aroutB[hf]], W=bufs,
              sembuf=aroutB[hf])

    def evac_out(j, psP, psB):
        A(lambda e: e.activation(oT_ap(j), psP[:, :], AF.Copy), R=[psB], W=oT_bufs(j))
        if j == 7:
            ar_half(0)
        elif j == 15:
            ar_half(1)

    def allreduce_out():
        for j in range(KC):
            sq_accum(oT_ap(j), oT_bufs(j)[0], j, j == 0, j == KC - 1)

    def post_norm(kind, l):
        allreduce_out()
        stats_rstd(D, 0)
        for c in range(KC):
            V(lambda e, c=c: e.scalar_tensor_tensor(oT_ap(c), oT_ap(c), gain_ap(kind, l, c),
                                                    rstd[:, 0, :], ALU.mult, ALU.mult),
              R=oT_bufs(c) + [rstdB[0], constB], W=oT_bufs(c))
            V(lambda e, c=c: e.tensor_tensor(xg[:, c, :], xg[:, c, :], oT_ap(c), ALU.add),
              R=oT_bufs(c) + [xgB[c]], W=[xgB[c]])

    def ffn(l):
        w2 = ffn_w_gu_d[l]
        for f in range(NFL):
            slab, sB = get_slab(w2, 0, D, [(f * 128, 128), (DFF // TP + f * 128, 128)])
            gP, gB = acc()
            mm(gP[:, :], gB, [(slab[:, kc, 0:128], hT[:, kc, :]) for kc in range(KC)], R=[sB] + hTB)
            uP, uB = acc()
            mm(uP[:, :], uB, [(slab[:, kc, 128:256], hT[:, kc, :]) for kc in range(KC)], R=[sB] + hTB)
            k = f % 2
            A(lambda e, gP=gP, k=k: e.activation(sq[:, k, :], gP[:, :], AF.Silu), R=[gB], W=[sqB[k]])
            V(lambda e, uP=uP, k=k, f=f: e.tensor_tensor(big[:, f, :], sq[:, k, :], uP[:, :], ALU.mult),
              R=[sqB[k], uB], W=[bigB[f]])
        wd = ffn_w_down_d[l]
        for j in range(KC):
            slA, bA = get_slab(wd, 0, 2816, [(j * 128, 128)])
            oP, oB = acc()
            pairs = [(slA[:, kc, :], big[:, kc, :]) for kc in range(NFL)]
            mm(oP[:, :], oB, pairs, R=[bA] + bigB)
            evac_out(j, oP, oB)

    cosT = tabs[:, 0, :]
    sinT = tabs[:, 1, :]
    t1 = tabs[:, 2, :]
    t2 = tabs[:, 3, :]
    tmpB = [Buf("t1"), Buf("t2")]

    def make_tables(g, mla):
        P.dma(sp, posi[:, :], pos_d[0:1, g * G:(g + 1) * G].partition_broadcast(128), W=[posB])
        V(lambda e: e.tensor_copy(posf[:, :], posi[:, :]), R=[posB], W=[posB])
        inv = inv_mla if mla else inv_ret
        for ti, shift in ((1, 0.0), (0, 0.25)):
            V(lambda e, ti=ti, shift=shift: e.tensor_scalar(tabs[:, ti, :], posf[:, :], inv, shift, ALU.mult, ALU.add),
              R=[posB, constB], W=[tabsB])
            V(lambda e, ti=ti: e.tensor_copy(posi[:, :], tabs[:, ti, :]), R=[tabsB], W=[posB])
            V(lambda e: e.tensor_copy(t1, posi[:, :]), R=[posB], W=[tmpB[0]])
            V(lambda e, ti=ti: e.tensor_tensor(tabs[:, ti, :], tabs[:, ti, :], t1, ALU.subtract), R=[tabsB, tmpB[0]], W=[tabsB])
            A(lambda e, ti=ti: e.activation(tabs[:, ti, :], tabs[:, ti, :], AF.Sin, scale=2 * PI), R=[tabsB], W=[tabsB])
        if mla:
            V(lambda e: e.tensor_scalar(tabs[:, 1, :], tabs[:, 1, :], sgn_mla, None, ALU.mult),
              R=[tabsB, constB], W=[tabsB])

    def retention(l, g):
        j = l // 2
        w_in = ret_w_in_d[j]
        o = [0]

        def carve(n_bf16):
            a = arena[:, o[0]:o[0] + n_bf16]
            o[0] += n_bf16
            return a
        qT = carve(2 * G).rearrange("p (c t) -> p c t", c=2)
        kT = carve(2 * G).rearrange("p (c t) -> p c t", c=2)
        vT = carve(4 * RDV).rearrange("p (c t) -> p c t", c=4)
        gT = carve(4 * RDV).rearrange("p (c t) -> p c t", c=4)
        ktok = carve(256)
        sTb = carve(128)
        Sbf = carve(2 * RDV).rearrange("p (c t) -> p c t", c=2)
        yg = carve(RDV)
        S32 = [carve(2 * 2 * RDV).bitcast(F32).rearrange("p (c t) -> p c t", c=2) for _ in range(2)]
        y32 = carve(2 * RDV).bitcast(F32)
        i32 = carve(2 * RDV).bitcast(F32)
        st6 = small[:, 0:6]
        mv = small[:, 6:8]
        rs = small[:, 8:9]
        qTB, kTB = PB("r_qT"), PB("r_kT")
        vTB = [PB("r_v%d" % i) for i in range(4)]
        gTB = [PB("r_g%d" % i) for i in range(4)]
        ktokB, sTbB, SbfB, ygB, y32B, i32B = PB("r_ktok"), PB("r_sTb"), PB("r_Sbf"), PB("r_yg"), PB("r_y32"), PB("r_i32")
        S32B = [PB("r_S32a"), PB("r_S32b")]
        if g == 0:
            for eng in (pe, act, dve, sp):
                P.barrier_wait(eng, [b for n, b in REG.items() if n.startswith("m_")])
        gC = [float(np.exp(np.float32(np.log1p(-np.exp2(np.float32(-5.0 - h)))) * np.float32(RC))) for h in range(RH)]

        for h in range(RHL):
            Sh, ShB = S32[h % 2], S32B[h % 2]
            if g == 0:
                V(lambda e, Sh=Sh: e.memset(Sh[:, :, :], 0.0), R=[], W=[ShB])
            else:
                P.dma(sp, Sh[:, :, :], ssc_d[h], R=[ssc_bufs[h]], W=[ShB])
            A(lambda e, Sh=Sh: e.activation(Sbf[:, :, :], Sh[:, :, :], AF.Copy), R=[ShB], W=[SbfB])

            for which, base, dst, dstB in ((0, h * 256, qT, qTB), (1, 1024 + h * 256, kT, kTB)):
                slab, sB = get_slab(w_in, 0, D, [(base, 256)])
                p0, b0 = acc()
                mm(p0[:, :], b0, [(slab[:, kc, 0:128], hT[:, kc, :]) for kc in range(KC)], R=[sB] + hTB)
                p1, b1 = acc()
                mm(p1[:, :], b1, [(slab[:, kc, 128:256], hT[:, kc, :]) for kc in range(KC)], R=[sB] + hTB)
                V(lambda e, p0=p0: e.tensor_tensor(t1, p0[:, :], cosT, ALU.mult), R=[b0, tabsB], W=[tmpB[0]])
                V(lambda e, p1=p1: e.tensor_tensor(t2, p1[:, :], sinT, ALU.mult), R=[b1, tabsB], W=[tmpB[1]])
                V(lambda e, dst=dst: e.tensor_tensor(dst[:, 0, :], t1, t2, ALU.subtract), R=tmpB, W=[dstB])
                V(lambda e, p0=p0: e.tensor_tensor(t1, p0[:, :], sinT, ALU.mult), R=[b0, tabsB], W=[tmpB[0]])
                V(lambda e, p1=p1: e.tensor_tensor(t2, p1[:, :], cosT, ALU.mult), R=[b1, tabsB], W=[tmpB[1]])
                V(lambda e, dst=dst: e.tensor_tensor(dst[:, 1, :], t1, t2, ALU.add), R=tmpB, W=[dstB])
            for which, base, dst, dstBs in ((0, 2048 + h * 512, vT, vTB), (1, 4096 + h * 512, gT, gTB)):
                for half in range(2):
                    slab, sB = get_slab(w_in, 0, D, [(base + half * 256, 256)])
                    for tt in range(4):
                        p0, b0 = acc()
                        mm(p0[:, 0:256], b0, [(hT[:, kc, tt * 128:(tt + 1) * 128], slab[:, kc, :]) for kc in range(KC)],
                           R=[sB] + hTB)
                        fn = AF.Copy if which == 0 else AF.Silu
                        A(lambda e, p0=p0, dst=dst, tt=tt, half=half, fn=fn:
                          e.activation(dst[:, tt, half * 256:(half + 1) * 256], p0[:, 0:256], fn),
                          R=[b0], W=[dstBs[tt]])
            for c in range(4):
                tok = slice(c * 128, (c + 1) * 128)
                sP, sBk = acc()
                mm(sP[:, 0:128], sBk, [(kT[:, dc, tok], qT[:, dc, tok]) for dc in range(2)], R=[kTB, qTB])
                V(lambda e, sP=sP, h=h: e.tensor_tensor(sTb, sP[:, 0:128], DinT[:, h, :], ALU.mult),
                  R=[sBk, constB], W=[sTbB])
                iaP, iaB = acc()
                mm(iaP[:, :], iaB, [(sTb, vT[:, c, :])], R=[sTbB, vTB[c]])
                ieP, ieB = acc()
                mm(ieP[:, :], ieB, [(qT[:, dc, tok], Sbf[:, dc, :]) for dc in range(2)], R=[qTB, SbfB])
                A(lambda e, ieP=ieP, h=h: e.activation(i32, ieP[:, :], AF.Copy, scale=dq_c[:, h:h + 1]),
                  R=[ieB, constB], W=[i32B])
                V(lambda e, iaP=iaP: e.tensor_tensor(y32, i32, iaP[:, :], ALU.add), R=[i32B, iaB], W=[y32B])
                V(lambda e: e.bn_stats(st6, y32), R=[y32B], W=[smallB])
                V(lambda e: e.bn_aggr(mv, st6), R=[smallB], W=[smallB])
                A(lambda e: e.activation(rs, mv[:, 1:2], AF.Sqrt, bias=EPS, scale=1.0), R=[smallB], W=[smallB])
                V(lambda e: e.reciprocal(rs, rs), R=[smallB], W=[smallB])
                V(lambda e: e.tensor_scalar(y32, y32, mv[:, 0:1], rs, ALU.subtract, ALU.mult),
                  R=[y32B, smallB], W=[y32B])
                V(lambda e, c=c: e.tensor_tensor(yg, y32, gT[:, c, :], ALU.mult), R=[y32B, gTB[c]], W=[ygB])
                for jj in range(4):
                    tr(tpb[:, jj * 128:(jj + 1) * 128], tpA_B, yg[:, jj * 128:(jj + 1) * 128], R=[ygB])
                for jj in range(4):
                    ch = h * 4 + jj
                    V(lambda e, jj=jj, ch=ch, tok=tok: e.tensor_scalar(
                        big[:, ch, tok], tpb[:, jj * 128:(jj + 1) * 128], gn[:, j * 16 + ch:j * 16 + ch + 1], None, ALU.mult),
                      R=[tpA_B, constB], W=[bigB[ch]])
                for dc in range(2):
                    tr(tp2[:, dc * 128:(dc + 1) * 128], a1B, kT[:, dc, tok], R=[kTB])
                V(lambda e, h=h: e.tensor_scalar(ktok, tp2[:, 0:256], dk_c[:, h:h + 1], None, ALU.mult),
                  R=[a1B, constB], W=[ktokB])
                for dc in range(2):
                    kvP, kvB = acc()
                    mm(kvP[:, :], kvB, [(ktok[:, dc * 128:(dc + 1) * 128], vT[:, c, :])], R=[ktokB, vTB[c]])
                    V(lambda e, kvP=kvP, dc=dc, Sh=Sh, h=h: e.scalar_tensor_tensor(
                        Sh[:, dc, :], Sh[:, dc, :], gC_c[:, h:h + 1], kvP[:, :], ALU.mult, ALU.add),
                      R=[ShB, kvB, constB], W=[ShB])
                if c < 3:
                    A(lambda e, Sh=Sh: e.activation(Sbf[:, :, :], Sh[:, :, :], AF.Copy), R=[ShB], W=[SbfB])
            if g < ngroups - 1:
                P.dma(sp, ssc_d[h], Sh[:, :, :], R=[ShB], W=[ssc_bufs[h]], sembuf=ShB)
        w_out = ret_w_out_d[j]
        for jo2 in range(KC // 2):
            slab, sB = get_slab(w_out, 0, 2048, [(jo2 * 256, 256)])
            for sub in range(2):
                jo = jo2 * 2 + sub
                oP, oB = acc()
                mm(oP[:, :], oB, [(slab[:, kc, sub * 128:(sub + 1) * 128], big[:, kc, :]) for kc in range(16)],
                   R=[sB] + bigB[0:16])
                evac_out(jo, oP, oB)

    ckvT = arena[:, 0:4 * S].rearrange("p (c t) -> p c t", c=4)
    kropeT = arena[:, 4 * S:5 * S]
    ckvB = [PB("m_ckv%d" % i) for i in range(NG)]
    kropeB = [PB("m_krope%d" % i) for i in range(NG)]

    def mla(l, g):
        j = l // 2
        o = [5 * S]

        def carve(n_bf16):
            a = arena[:, o[0]:o[0] + n_bf16]
            o[0] += n_bf16
            return a
        cqT = carve(4 * G).rearrange("p (c t) -> p c t", c=4)
        qnT = carve(G)
        qrT = carve(G)
        KhT = carve(S)
        Vh = carve(16 * 128).rearrange("p (k v) -> p k v", k=16)
        PT = carve(2 * G).rearrange("p (k t) -> p k t", k=2)
        rec = carve(2 * G).bitcast(F32)
        assert o[0] <= 40 * 512
        cqB, qnB, qrB, KhB, VhB, recB = PB("m_cq"), PB("m_qn"), PB("m_qr"), PB("m_Kh"), PB("m_Vh"), PB("m_rec")
        PTB = [PB("m_PT0"), PB("m_PT1")]
        if g == 0:
            for eng in (pe, act, dve, sp):
                P.barrier_wait(eng, [b for n, b in REG.items() if n.startswith("r_")])
        w_in = mla_w_in_d[j]
        scale = float((MN + MR) ** -0.5)
        for t in range(8):
            if t % 2 == 0:
                slab, sB = get_slab(w_in, 0, D, [(t * 128, 256)])
            off = (t % 2) * 128
            p0, b0 = acc()
            mm(p0[:, :], b0, [(slab[:, kc, off:off + 128], hT[:, kc, :]) for kc in range(KC)], R=[sB] + hTB)
            A(lambda e, t=t, p0=p0: e.activation(oT_ap(8 + t), p0[:, :], AF.Copy), R=[b0], W=oT_bufs(8 + t))
            if t < 4:
                sq_accum(p0[:, :], b0, t, t == 0, t == 3, ssP, ssB)
            else:
                sq_accum(p0[:, :], b0, t, t == 4, t == 7, a1P, a1B)
        stats_rstd(MQL, 0)
        stats_rstd(MKL, 1, a1P, a1B)
        for c in range(4):
            gi = (j * 2 + 0) * 4 + c
            V(lambda e, c=c, gi=gi: e.scalar_tensor_tensor(cqT[:, c, :], oT_ap(8 + c), gqkv[:, gi:gi + 1],
                                                           rstd[:, 0, :], ALU.mult, ALU.mult),
              R=oT_bufs(8 + c) + [rstdB[0], constB], W=[cqB])
            gi2 = (j * 2 + 1) * 4 + c
            V(lambda e, c=c, gi2=gi2: e.scalar_tensor_tensor(ckvT[:, c, g * G:(g + 1) * G], oT_ap(12 + c),
                                                             gqkv[:, gi2:gi2 + 1], rstd[:, 1, :], ALU.mult, ALU.mult),
              R=oT_bufs(12 + c) + [rstdB[1], constB], W=[ckvB[g]])
        slab, sB = get_slab(w_in, 0, D, [(1024, 128)])
        pA, bA = acc()
        mm(pA[0:64, :], bA, [(slab[:, kc, 0:64], hT[:, kc, :]) for kc in range(KC)], R=[sB] + hTB)
        pB, bB = acc()
        mm(pB[0:64, :], bB, [(slab[:, kc, 64:128], hT[:, kc, :]) for kc in range(KC)], R=[sB] + hTB)
        V(lambda e, pA=pA: e.tensor_tensor(t1[0:64, :], pA[0:64, :], cosT[0:64, :], ALU.mult), R=[bA, tabsB], W=[tmpB[0]])
        V(lambda e, pB=pB: e.tensor_tensor(t2[0:64, :], pB[0:64, :], sinT[0:64, :], ALU.mult), R=[bB, tabsB], W=[tmpB[1]])
        V(lambda e: e.tensor_tensor(kropeT[0:64, g * G:(g + 1) * G], t1[0:64, :], t2[0:64, :], ALU.add),
          R=tmpB, W=[kropeB[g]])
        nkt = 4 * (g + 1)
        for h in range(MHL):
            slq, sqB_ = get_slab(mla_w_uq_d[j], 0, MQL, [(h * 256, 256)])
            p0, b0 = acc()
            mm(p0[:, :], b0, [(slq[:, kc, 0:128], cqT[:, kc, :]) for kc in range(4)], R=[sqB_, cqB])
            A(lambda e, p0=p0: e.activation(qnT, p0[:, :], AF.Copy), R=[b0], W=[qnB])
            pA, bA = acc()
            mm(pA[0:64, :], bA, [(slq[:, kc, 128:192], cqT[:, kc, :]) for kc in range(4)], R=[sqB_, cqB])
            pB, bB = acc()
            mm(pB[0:64, :], bB, [(slq[:, kc, 192:256], cqT[:, kc, :]) for kc in range(4)], R=[sqB_, cqB])
            V(lambda e, pA=pA: e.tensor_tensor(t1[0:64, :], pA[0:64, :], cosT[0:64, :], ALU.mult), R=[bA, tabsB], W=[tmpB[0]])
            V(lambda e, pB=pB: e.tensor_tensor(t2[0:64, :], pB[0:64, :], sinT[0:64, :], ALU.mult), R=[bB, tabsB], W=[tmpB[1]])
            V(lambda e: e.tensor_tensor(qrT[0:64, :], t1[0:64, :], t2[0:64, :], ALU.add), R=tmpB, W=[qrB])
            slkv, skvB = get_slab(mla_w_ukv_d[j], 0, MKL, [(h * 256, 256)])
            for kg in range(g + 1):
                p0, b0 = acc()
                mm(p0[:, :], b0, [(slkv[:, kc, 0:128], ckvT[:, kc, kg * G:(kg + 1) * G]) for kc in range(4)],
                   R=[skvB, ckvB[kg]])
                A(lambda e, p0=p0, kg=kg: e.activation(KhT[:, kg * G:(kg + 1) * G], p0[:, :], AF.Copy), R=[b0], W=[KhB])
                p1, b1 = acc()
                for kt in range(4):
                    ktg = kg * 4 + kt
                    mm(p1[:, kt * 128:(kt + 1) * 128], b1,
                       [(ckvT[:, kc, ktg * 128:(ktg + 1) * 128], slkv[:, kc, 128:256]) for kc in range(4)],
                       R=[skvB, ckvB[kg]])
                V(lambda e, p1=p1, kg=kg: e.tensor_copy(
                    Vh[:, kg * 4:(kg + 1) * 4, :], p1[:, :].rearrange("p (k v) -> p k v", k=4)), R=[b1], W=[VhB])
            for kt in range(nkt):
                jd = kt - 4 * g
                kgi = kt // 4
                sP, sBk = acc()
                mm(sP[:, :], sBk, [(KhT[:, kt * 128:(kt + 1) * 128], qnT),
                                   (kropeT[0:64, kt * 128:(kt + 1) * 128], qrT[0:64, :])],
                   R=[KhB, qnB, qrB, kropeB[kgi]])
                k = kt % 2
                A(lambda e, sP=sP, k=k: e.activation(PT[:, k, :], sP[:, :], AF.Exp, scale=scale), R=[sBk], W=[PTB[k]])
                if jd >= 0:
                    V(lambda e, k=k, jd=jd: e.tensor_tensor(PT[:, k, :], PT[:, k, :], maskF[:, jd, :], ALU.mult),
                      R=[PTB[k], constB], W=[PTB[k]])
                mm(a1P[:, :], a1B, [(Vh[:, kt, :], PT[:, k, :])], R=[VhB, PTB[k]], start=(kt == 0), stop=(kt == nkt - 1))
                mm(a2P[:, :], a2B, [(ones_b, PT[:, k, :])], R=[constB, PTB[k]], start=(kt == 0), stop=(kt == nkt - 1))
            V(lambda e: e.reciprocal(rec, a2P[:, :]), R=[a2B], W=[recB])
            V(lambda e, h=h: e.tensor_tensor(big[:, h, :], a1P[:, :], rec, ALU.mult), R=[a1B, recB], W=[bigB[h]])
        w_out = mla_w_out_d[j]
        for jo2 in range(KC // 2):
            slab, sB = get_slab(w_out, 0, 1024, [(jo2 * 256, 256)])
            for sub in range(2):
                jo = jo2 * 2 + sub
                oP, oB = acc()
                mm(oP[:, :], oB, [(slab[:, kc, sub * 128:(sub + 1) * 128], big[:, kc, :]) for kc in range(MHL)],
                   R=[sB] + bigB[0:MHL])
                evac_out(jo, oP, oB)

    all_arena = [Buf("arena_guard")]
    for l in range(nlayers):
        is_mla = (l % 2 == 1)
        if dbg == 'allret':
            is_mla = False
        if dbg == 'allmla':
            is_mla = True
        for g in range(ngroups):
            src = xT_d if l == 0 else xs_d
            srcB = [] if l == 0 else [xs_bufs[g]]
            P.dma(sp, xg[:, :, :], src[:, :, g * G:(g + 1) * G].rearrange("c p t -> p c t"),
                  R=srcB, W=xgB)
            make_tables(g, is_mla)
            pre_norm(0, l)
            if is_mla:
                mla(l, g)
            else:
                retention(l, g)
            post_norm(1, l)
            pre_norm(2, l)
            ffn(l)
            post_norm(3, l)
            last = (l == nlayers - 1)
            dst = outT_d if last else xs_d
            dstB = out_bufs[g] if last else xs_bufs[g]
            P.dma(sp, dst[:, :, g * G:(g + 1) * G].rearrange("c p t -> p c t"), xg[:, :, :], R=xgB, W=[dstB], sembuf=xgB[0])
    P.barrier_wait(sp, out_bufs)

    sems = {}
    for e in P.engs:
        sems[e.name] = es.enter_context(nc.semaphore("s_" + e.name))
    for k in P.dma_sems:
        sems[k] = es.enter_context(nc.semaphore("s_" + k))
    block = es.enter_context(nc.Block())

    def replay(eng, handle):
        for it in eng.q:
            if it[0] == "wait":
                handle.wait_ge(sems[it[1]], it[2])
            elif it[0] == "op":
                ins = it[1](handle)
                ins.then_inc(sems[eng.name], 1)
            else:
                handle.dma_start(out=it[1], in_=it[2]).then_inc(sems[it[3]], 16)

    @block.tensor
    def _(e):
        replay(pe, e)

    @block.scalar
    def _(e):
        replay(act, e)

    @block.vector
    def _(e):
        replay(dve, e)

    @block.gpsimd
    def _(e):
        replay(pool, e)

    @block.sync
    def _(e):
        replay(sp, e)

    es.close()
    return nc


def _consts(par):
    f32 = np.float32
    hh = (np.arange(RHL) + par * RHL).astype(f32)
    log_gamma = np.log1p(-np.exp2(-5.0 - hh)).astype(f32)
    idx = np.arange(RC, dtype=f32)
    k = idx[:, None]
    q = idx[None, :]
    rel = q - k
    DinT = np.where(rel[None] >= 0, np.exp(log_gamma[:, None, None] * np.maximum(rel, 0)[None]), 0.0).astype(f32) / 16.0
    dq = np.exp(log_gamma[None, :] * (idx[:, None] + 1.0)).astype(f32)
    dk = (np.exp(log_gamma[None, :] * (RC - 1.0 - idx[:, None])) / 16.0).astype(f32)
    gC = np.exp(log_gamma * f32(RC)).astype(f32)
    p = np.arange(128)
    inv_ret = (THETA ** (-(np.arange(0, 256, 2, dtype=f32)) / f32(256))).astype(f32)
    inv32 = (THETA ** (-(np.arange(0, 64, 2, dtype=f32)) / f32(64))).astype(f32)
    inv_mla = inv32[p % 32]
    sgn = np.where((p % 64) < 32, -1.0, 1.0).astype(f32)
    cf = np.zeros((128, 8 * 128 + 8 + 8 + 4 + 128), f32)
    cf[:, 0:512] = DinT.transpose(1, 0, 2).reshape(128, 512)
    cf[:, 512:516] = gC[None, :]
    cf[:, 1024:1028] = dq
    cf[:, 1032:1036] = dk
    cf[:, 1040] = inv_ret / f32(2 * np.pi)
    cf[:, 1041] = inv_mla / f32(2 * np.pi)
    cf[:, 1042] = sgn
    cf[:, 1044:1044 + 128] = 1.0
    cb = np.zeros((128, 128 + 128 + 2048), f32)
    cb[:, 0:128] = np.eye(128)
    cb[:, 128:256] = 1.0
    kk = np.arange(128)[:, None]
    qq = np.arange(512)[None, :]
    for jd in range(4):
        cb[:, 256 + jd * 512:256 + (jd + 1) * 512] = (qq >= jd * 128 + kk)
    return cf, cb.astype(ml_dtypes.bfloat16)


def _fm(v):
    return np.ascontiguousarray(v.reshape(-1, 128).T)


_NC_CACHE = {}


def kernel(x, positions, norm_mix_pre, norm_mix_post, norm_ffn_pre, norm_ffn_post,
           ret_w_in, ret_gn_g, ret_w_out,
           mla_w_in, mla_g_q, mla_g_kv, mla_w_uq, mla_w_ukv, mla_w_out,
           ffn_w_gu, ffn_w_down):
    f32 = np.float32
    x = np.asarray(x, f32)
    c = np.ascontiguousarray
    gains = np.concatenate([_fm(np.asarray(a, f32)[l]) for a in (norm_mix_pre, norm_mix_post, norm_ffn_pre, norm_ffn_post)
                            for l in range(L)], axis=1)
    gqkv = np.concatenate([_fm(np.asarray(a, f32)[j]) for j in range(2) for a in (mla_g_q, mla_g_kv)], axis=1)
    mla_w_in = np.asarray(mla_w_in, f32)
    w_in_ext = c(np.concatenate([mla_w_in, mla_w_in[:, :, 1056:1088], mla_w_in[:, :, 1024:1056]], axis=2))
    wq = np.asarray(mla_w_uq, f32).reshape(2, MQL, MH, MN + MR)
    wq_ext = np.concatenate([wq, wq[..., MN + 32:MN + 64], wq[..., MN:MN + 32]], axis=3)
    ret_w_in = np.asarray(ret_w_in, f32)
    ret_w_out = np.asarray(ret_w_out, f32)
    ret_gn_g = np.asarray(ret_gn_g, f32)
    mla_w_ukv = np.asarray(mla_w_ukv, f32).reshape(2, MKL, MH, 256)
    mla_w_out = np.asarray(mla_w_out, f32)
    ffn_w_gu = np.asarray(ffn_w_gu, f32)
    ffn_w_down = np.asarray(ffn_w_down, f32)
    per_par = []
    for par in range(TP):
        cf, cb = _consts(par)
        hs = slice(par * RHL, (par + 1) * RHL)
        q_ = ret_w_in[:, :, 0:2048].reshape(2, D, RH, 256)[:, :, hs].reshape(2, D, -1)
        k_ = ret_w_in[:, :, 2048:4096].reshape(2, D, RH, 256)[:, :, hs].reshape(2, D, -1)
        v_ = ret_w_in[:, :, 4096:8192].reshape(2, D, RH, 512)[:, :, hs].reshape(2, D, -1)
        g_ = ret_w_in[:, :, 8192:12288].reshape(2, D, RH, 512)[:, :, hs].reshape(2, D, -1)
        ms = slice(par * MHL, (par + 1) * MHL)
        hid = slice(par * (DFF // TP), (par + 1) * (DFF // TP))
        per_par.append({
            "cf": cf, "cb": cb,
            "gn": c(np.concatenate([_fm(ret_gn_g[j][par * 2048:(par + 1) * 2048]) for j in range(2)], axis=1)),
            "ret_w_in": c(np.concatenate([q_, k_, v_, g_], axis=2)),
            "ret_w_out": c(ret_w_out[:, par * 2048:(par + 1) * 2048, :]),
            "mla_w_uq": c(wq_ext[:, :, ms].reshape(2, MQL, -1)),
            "mla_w_ukv": c(mla_w_ukv[:, :, ms].reshape(2, MKL, -1)),
            "mla_w_out": c(mla_w_out[:, par * 1024:(par + 1) * 1024, :]),
            "ffn_w_gu": c(np.concatenate([ffn_w_gu[:, :, 0:DFF][:, :, hid], ffn_w_gu[:, :, DFF:][:, :, hid]], axis=2)),
            "ffn_w_down": c(ffn_w_down[:, hid, :]),
        })
    shared = {"gains": c(gains), "gqkv": c(gqkv), "mla_w_in": w_in_ext}
    in_maps = []
    for core in range(NCORE):
        b, par = core // TP, core % TP
        m = dict(shared)
        m.update(per_par[par])
        m["xT"] = c(x[b].T).reshape(KC, 128, S)
        m["pos"] = c(np.asarray(positions)[b].astype(np.int32).reshape(1, S))
        in_maps.append(m)
    if "nc" not in _NC_CACHE:
        _NC_CACHE["nc"] = build()
    res = run_bass_kernel_spmd(_NC_CACHE["nc"], in_maps, core_ids=list(range(NCORE)))
    out = np.stack([np.asarray(res.results[b * TP]["outT"]).reshape(D, S).T for b in range(NB)], axis=0)
    return np.ascontiguousarray(out.astype(f32))
```

```python
import numpy as np
import ml_dtypes
import concourse.bass as bass
import concourse.mybir as mybir
from concourse.bass_utils import run_bass_kernel_spmd

F32 = mybir.dt.float32
BF16 = mybir.dt.bfloat16
I32 = mybir.dt.int32
ALU = mybir.AluOpType
AF = mybir.ActivationFunctionType

D = 2048
S = 2048
NB = 4
TP = 2
NCORE = NB * TP
L = 4
G = 512
NG = S // G
KC = D // 128
DFF = 5632
NF = DFF // 128
NFL = NF // TP
RH, RDK, RDV, RC = 8, 256, 512, 128
RHL = RH // TP
MHL = 16 // TP
MH, MN, MR, MV, MQL, MKL = 16, 128, 64, 128, 512, 512
EPS = 1e-6
THETA = 10000.0
PI = float(np.pi)
SLOT = 4096
NSLOT = 5
SAME_ENGINE_SYNC = True


class Buf:
    __slots__ = ("name", "w", "r", "dsem")

    def __init__(self, name):
        self.name = name
        self.w = None
        self.r = {}
        self.dsem = None


class Eng:
    def __init__(self, name, is_pe=False):
        self.name = name
        self.q = []
        self.cnt = 0
        self.seen = {}
        self.is_pe = is_pe


class Prog:
    def __init__(self, nc):
        self.nc = nc
        self.pe = Eng("pe", True)
        self.act = Eng("act")
        self.dve = Eng("dve")
        self.pool = Eng("pool")
        self.sp = Eng("sp")
        self.engs = [self.pe, self.act, self.dve, self.pool, self.sp]
        self.dma_sems = {}
        self.nd = 0

    def _deps(self, eng, R, W):
        deps = {}

        def add(d):
            if d is None:
                return
            k, v = d
            if deps.get(k, 0) < v:
                deps[k] = v
        for b in R:
            add(b.w)
        for b in W:
            add(b.w)
            for k, v in b.r.items():
                add((k, v))
        for k, v in deps.items():
            if k == eng.name and (eng.is_pe or not SAME_ENGINE_SYNC):
                continue
            if eng.seen.get(k, 0) < v:
                eng.q.append(("wait", k, v))
                eng.seen[k] = v

    def op(self, eng, fn, R=(), W=()):
        self._deps(eng, R, W)
        eng.cnt += 1
        me = (eng.name, eng.cnt)
        eng.q.append(("op", fn))
        for b in R:
            if b.r.get(me[0], 0) < me[1]:
                b.r[me[0]] = me[1]
        for b in W:
            b.w = me
            b.r = {}

    def dma(self, eng, out_ap, in_ap, R=(), W=(), sembuf=None):
        self._deps(eng, R, W)
        sb = sembuf if sembuf is not None else (W[0] if W else R[0])
        if sb.dsem is None:
            sb.dsem = "d%d" % self.nd
            self.nd += 1
            self.dma_sems[sb.dsem] = 0
        self.dma_sems[sb.dsem] += 16
        me = (sb.dsem, self.dma_sems[sb.dsem])
        eng.q.append(("dma", out_ap, in_ap, sb.dsem))
        for b in R:
            if b.r.get(me[0], 0) < me[1]:
                b.r[me[0]] = me[1]
        for b in W:
            b.w = me
            b.r = {}

    def barrier_wait(self, eng, bufs):
        self._deps(eng, (), bufs)


def build(nlayers=L, ngroups=NG, dbg=None):
    nc = bass.Bass("TRN2", target_bir_lowering=False)
    P = Prog(nc)
    pe, act, dve, pool, sp = P.pe, P.act, P.dve, P.pool, P.sp

    def din(name, shape, dt=F32):
        return nc.dram_tensor(name, list(shape), dt, kind="ExternalInput")
    xT_d = din("xT", [KC, 128, S])
    pos_d = din("pos", [1, S], I32)
    gains_d = din("gains", [128, 16 * KC])
    gn_d = din("gn", [128, 32])
    gqkv_d = din("gqkv", [128, 16])
    cf_d = din("cf", [128, 8 * 128 + 8 + 8 + 4 + 128])
    cb_d = din("cb", [128, 128 + 128 + 4 * 512], BF16)
    ret_w_in_d = din("ret_w_in", [2, 24, 128, 4096])
    ret_w_out_d = din("ret_w_out", [2, 8, 128, 4096])
    mla_w_in_d = din("mla_w_in", [2, 5, 128, 4096])
    mla_w_uq_d = din("mla_w_uq", [2, 8, 128, 1024])
    mla_w_ukv_d = din("mla_w_ukv", [2, 8, 128, 1024])
    mla_w_out_d = din("mla_w_out", [2, 8, 128, 2048])
    ffn_w_gu_d = din("ffn_w_gu", [L, NFL, 128, 4096])
    ffn_w_down_d = din("ffn_w_down", [L, KC, 128, NFL * 128])
    outT_d = nc.dram_tensor("outT", [KC, 128, S], F32, kind="ExternalOutput")
    xs_d = nc.dram_tensor("xs", [KC, 128, S], F32)
    ssc_d = nc.dram_tensor("ssc", [RHL, 128, 2, RDV], F32)
    arin_d = [nc.dram_tensor("arin%d" % i, [8 * 128, G], F32) for i in range(2)]
    arout_d = [nc.dram_tensor("arout%d" % i, [8 * 128, G], F32) for i in range(2)]
    arinB = [Buf("arin0"), Buf("arin1")]
    aroutB = [Buf("arout0"), Buf("arout1")]
    xs_bufs = [Buf("xs%d" % g) for g in range(NG)]
    ssc_bufs = [Buf("ssc%d" % h) for h in range(RH)]
    out_bufs = [Buf("out%d" % g) for g in range(NG)]

    from contextlib import ExitStack
    es = ExitStack()

    def sb(name, shape, dt):
        return es.enter_context(nc.sbuf_tensor("sb_" + name, list(shape), dt))

    def ps(name, shape, dt):
        return es.enter_context(nc.psum_tensor(name, list(shape), dt))

    xg = sb("xg", [128, KC, G], F32)
    hoT = sb("hoT", [128, 2 * KC * G], BF16)
    big = sb("big", [128, NFL, G], BF16)
    ring = sb("ring", [128, NSLOT, SLOT], BF16)
    arena = sb("arena", [128, 40 * 512], BF16)
    sq = sb("sq", [128, 2, G], F32)
    rstd = sb("rstd", [128, 2, G], F32)
    tabs = sb("tabs", [128, 4, G], F32)
    posi = sb("posi", [128, G], I32)
    posf = sb("posf", [128, G], F32)
    gains = sb("gains", [128, 16 * KC], F32)
    gn = sb("gn", [128, 32], F32)
    gqkv = sb("gqkv", [128, 16], F32)
    cf = sb("cf", [128, 8 * 128 + 8 + 8 + 4 + 128], F32)
    cb = sb("cb", [128, 128 + 128 + 4 * 512], BF16)
    small = sb("small", [128, 16], F32)

    hT = hoT[:, 0:KC * G].rearrange("p (c t) -> p c t", c=KC)
    oT = hoT.bitcast(F32).rearrange("p (c t) -> p c t", c=KC) if hasattr(hoT, "bitcast") else None
    DinT = cf[:, 0:512].rearrange("p (h q) -> p h q", h=RHL)
    gC_c = cf[:, 512:516]
    dq_c = cf[:, 1024:1032]
    dk_c = cf[:, 1032:1040]
    inv_ret = cf[:, 1040:1041]
    inv_mla = cf[:, 1041:1042]
    sgn_mla = cf[:, 1042:1043]
    ones_f = cf[:, 1044:1044 + 128]
    ident_b = cb[:, 0:128]
    ones_b = cb[:, 128:256]
    maskF = cb[:, 256:256 + 2048].rearrange("p (j q) -> p j q", j=4)

    banks = [ps("pb%d" % i, [128, 512], F32) for i in range(7)]
    tpb = ps("tpb", [128, 1024], BF16)
    bankB = [Buf("bank%d" % i) for i in range(7)]
    tpA_B, tpB_B = Buf("tpA"), Buf("tpB")
    acc_i = [0]

    def acc():
        i = acc_i[0] % 4
        acc_i[0] += 1
        return banks[i], bankB[i]
    ssP, ssB = banks[4], bankB[4]
    a1P, a1B = banks[5], bankB[5]
    a2P, a2B = banks[6], bankB[6]
    tp2 = banks[5][:, :].bitcast(BF16)

    REG = {}

    def PB(name):
        if name not in REG:
            REG[name] = Buf(name)
        return REG[name]
    xgB = [Buf("xg%d" % c) for c in range(KC)]
    hTB = [Buf("hT%d" % c) for c in range(KC)]
    bigB = [Buf("big%d" % f) for f in range(NFL)]
    ringB = [Buf("ring%d" % i) for i in range(NSLOT)]
    sqB = [Buf("sq0"), Buf("sq1")]
    rstdB = [Buf("rstd0"), Buf("rstd1")]
    tabsB = Buf("tabs")
    posB = Buf("pos")
    constB = Buf("const")
    smallB = Buf("small")

    oTB_extra = [Buf("oTx%d" % j) for j in range(KC)]

    def oT_bufs(j):
        if j < 8:
            return [hTB[2 * j], hTB[2 * j + 1]]
        return [oTB_extra[j]]

    def oT_ap(j):
        return hoT[:, j * 1024:(j + 1) * 1024].bitcast(F32)

    ring_i = [0]

    def get_slab(dram2d, r0, nrows, col_ranges):
        i = ring_i[0] % NSLOT
        ring_i[0] += 1
        kc = nrows // 128
        ncols = sum(n for _, n in col_ranges)
        assert kc * ncols <= SLOT
        view = ring[:, i, 0:kc * ncols].rearrange("p (k n) -> p k n", k=kc)
        off = 0
        for (c0, n) in col_ranges:
            src = dram2d[r0:r0 + nrows, c0:c0 + n].rearrange("(k p) n -> p k n", p=128)
            P.dma(pool, view[:, :, off:off + n], src, W=[ringB[i]])
            off += n
        return view, ringB[i]

    def get_slab_t(dram_l, idx, kc, ncols):
        i = ring_i[0] % NSLOT
        ring_i[0] += 1
        assert kc * ncols <= SLOT
        flat = ring[:, i, 0:kc * ncols]
        P.dma(pool, flat, dram_l[idx], W=[ringB[i]])
        return flat.rearrange("p (k n) -> p k n", k=kc), ringB[i]

    def mm(out_ap, outB, pairs, R, start=True, stop=True):
        def fn(e, out_ap=out_ap, pairs=pairs, start=start, stop=stop):
            ins = None
            n = len(pairs)
            for i, (l, r) in enumerate(pairs):
                ins = e.matmul(out_ap, l, r, start=(start and i == 0), stop=(stop and i == n - 1))
            return ins
        P.op(pe, fn, R=R, W=[outB])

    def tr(out_ap, outB, in_ap, R):
        P.op(pe, lambda e: e.transpose(out_ap, in_ap, ident_b), R=list(R) + [constB], W=[outB])

    def A(fn, R, W):
        P.op(act, fn, R=R, W=W)

    def V(fn, R, W):
        P.op(dve, fn, R=R, W=W)

    for dst, src in ((gains, gains_d), (gn, gn_d), (gqkv, gqkv_d), (cf, cf_d), (cb, cb_d)):
        P.dma(sp, dst[:, :], src[:, :], W=[constB])

    def gain_ap(kind, l, c):
        i = (kind * 4 + l) * KC + c
        return gains[:, i:i + 1]

    def stats_rstd(n_feat, k, srcP=None, srcB=None):
        srcP = ssP if srcP is None else srcP
        srcB = ssB if srcB is None else srcB
        A(lambda e: e.activation(rstd[:, k, :], srcP[:, :], AF.Sqrt, bias=EPS, scale=1.0 / n_feat),
          R=[srcB], W=[rstdB[k]])
        V(lambda e: e.reciprocal(rstd[:, k, :], rstd[:, k, :]), R=[rstdB[k]], W=[rstdB[k]])

    def sq_accum(src_ap, srcB, idx, first, last, ssP_=None, ssB_=None):
        ssP_ = ssP if ssP_ is None else ssP_
        ssB_ = ssB if ssB_ is None else ssB_
        k = idx % 2
        A(lambda e: e.activation(sq[:, k, :], src_ap, AF.Square), R=[srcB], W=[sqB[k]])
        mm(ssP_[:, :], ssB_, [(ones_f, sq[:, k, :])], R=[sqB[k], constB], start=first, stop=last)

    def pre_norm(kind, l):
        for c in range(KC):
            sq_accum(xg[:, c, :], xgB[c], c, c == 0, c == KC - 1)
        stats_rstd(D, 0)
        for c in range(KC):
            V(lambda e, c=c: e.scalar_tensor_tensor(hT[:, c, :], xg[:, c, :], gain_ap(kind, l, c),
                                                    rstd[:, 0, :], ALU.mult, ALU.mult),
              R=[xgB[c], rstdB[0], constB], W=[hTB[c]])

    def ar_half(hf):
        bufs = []
        for jj_ in range(hf * 8, hf * 8 + 8):
            for b_ in oT_bufs(jj_):
                if b_ not in bufs:
                    bufs.append(b_)
        view = hoT[:, hf * 8192:(hf + 1) * 8192].bitcast(F32).rearrange("p (c t) -> p c t", c=8)
        P.dma(sp, arin_d[hf][:, :].rearrange("(c p) t -> p c t", p=128), view, R=bufs, W=[arinB[hf]])
        P.op(pool, lambda e: e.collective_compute("AllReduce", ALU.add, replica_groups=[[0, 1], [2, 3], [4, 5], [6, 7]],
                                                  ins=[arin_d[hf].ap().opt()], outs=[arout_d[hf].ap().opt()]),
             R=[arinB[hf]], W=[aroutB[hf]])
        P.dma(sp, view, arout_d[hf][:, :].rearrange("(c p) t -> p c t", p=128), R=[aroutB[hf]], W=bufs,
              sembuf=aroutB[hf])

    def evac_out(j, psP, psB):
        A(lambda e: e.activation(oT_ap(j), psP[:, :], AF.Copy), R=[psB], W=oT_bufs(j))
        if j == 7:
            ar_half(0)
        elif j == 15:
            ar_half(1)

    def allreduce_out():
        for j in range(KC):
            sq_accum(oT_ap(j), oT_bufs(j)[0], j, j == 0, j == KC - 1)

    def post_norm(kind, l):
        allreduce_out()
        stats_rstd(D, 0)
        for c in range(KC):
            V(lambda e, c=c: e.scalar_tensor_tensor(oT_ap(c), oT_ap(c), gain_ap(kind, l, c),
                                                    rstd[:, 0, :], ALU.mult, ALU.mult),
              R=oT_bufs(c) + [rstdB[0], constB], W=oT_bufs(c))
            V(lambda e, c=c: e.tensor_tensor(xg[:, c, :], xg[:, c, :], oT_ap(c), ALU.add),
              R=oT_bufs(c) + [xgB[c]], W=[xgB[c]])

    def ffn(l):
        w2 = ffn_w_gu_d[l]
        for f in range(NFL):
            slab, sB = get_slab_t(w2, f, KC, 256)
            gP, gB = acc()
            mm(gP[:, :], gB, [(slab[:, kc, 0:128], hT[:, kc, :]) for kc in range(KC)], R=[sB] + hTB)
            uP, uB = acc()
            mm(uP[:, :], uB, [(slab[:, kc, 128:256], hT[:, kc, :]) for kc in range(KC)], R=[sB] + hTB)
            k = f % 2
            A(lambda e, gP=gP, k=k: e.activation(sq[:, k, :], gP[:, :], AF.Silu), R=[gB], W=[sqB[k]])
            V(lambda e, uP=uP, k=k, f=f: e.tensor_tensor(big[:, f, :], sq[:, k, :], uP[:, :], ALU.mult),
              R=[sqB[k], uB], W=[bigB[f]])
        wd = ffn_w_down_d[l]
        for j in range(KC):
            slA, bA = get_slab_t(wd, j, NFL, 128)
            oP, oB = acc()
            pairs = [(slA[:, kc, :], big[:, kc, :]) for kc in range(NFL)]
            mm(oP[:, :], oB, pairs, R=[bA] + bigB)
            evac_out(j, oP, oB)

    cosT = tabs[:, 0, :]
    sinT = tabs[:, 1, :]
    t1 = tabs[:, 2, :]
    t2 = tabs[:, 3, :]
    tmpB = [Buf("t1"), Buf("t2")]

    def make_tables(g, mla):
        P.dma(sp, posi[:, :], pos_d[0:1, g * G:(g + 1) * G].partition_broadcast(128), W=[posB])
        V(lambda e: e.tensor_copy(posf[:, :], posi[:, :]), R=[posB], W=[posB])
        inv = inv_mla if mla else inv_ret
        for ti, shift in ((1, 0.0), (0, 0.25)):
            V(lambda e, ti=ti, shift=shift: e.tensor_scalar(tabs[:, ti, :], posf[:, :], inv, shift, ALU.mult, ALU.add),
              R=[posB, constB], W=[tabsB])
            V(lambda e, ti=ti: e.tensor_copy(posi[:, :], tabs[:, ti, :]), R=[tabsB], W=[posB])
            V(lambda e: e.tensor_copy(t1, posi[:, :]), R=[posB], W=[tmpB[0]])
            V(lambda e, ti=ti: e.tensor_tensor(tabs[:, ti, :], tabs[:, ti, :], t1, ALU.subtract), R=[tabsB, tmpB[0]], W=[tabsB])
            A(lambda e, ti=ti: e.activation(tabs[:, ti, :], tabs[:, ti, :], AF.Sin, scale=2 * PI), R=[tabsB], W=[tabsB])
        if mla:
            V(lambda e: e.tensor_scalar(tabs[:, 1, :], tabs[:, 1, :], sgn_mla, None, ALU.mult),
              R=[tabsB, constB], W=[tabsB])

    def retention(l, g):
        j = l // 2
        w_in = ret_w_in_d[j]
        o = [0]

        def carve(n_bf16):
            a = arena[:, o[0]:o[0] + n_bf16]
            o[0] += n_bf16
            return a
        qT = carve(2 * G).rearrange("p (c t) -> p c t", c=2)
        kT = carve(2 * G).rearrange("p (c t) -> p c t", c=2)
        vT = carve(4 * RDV).rearrange("p (c t) -> p c t", c=4)
        gT = carve(4 * RDV).rearrange("p (c t) -> p c t", c=4)
        ktok = carve(256)
        sTb = carve(128)
        Sbf = carve(2 * RDV).rearrange("p (c t) -> p c t", c=2)
        yg = carve(RDV)
        S32 = [carve(2 * 2 * RDV).bitcast(F32).rearrange("p (c t) -> p c t", c=2) for _ in range(2)]
        y32 = carve(2 * RDV).bitcast(F32)
        i32 = carve(2 * RDV).bitcast(F32)
        st6 = small[:, 0:6]
        mv = small[:, 6:8]
        rs = small[:, 8:9]
        qTB, kTB = PB("r_qT"), PB("r_kT")
        vTB = [PB("r_v%d" % i) for i in range(4)]
        gTB = [PB("r_g%d" % i) for i in range(4)]
        ktokB, sTbB, SbfB, ygB, y32B, i32B = PB("r_ktok"), PB("r_sTb"), PB("r_Sbf"), PB("r_yg"), PB("r_y32"), PB("r_i32")
        S32B = [PB("r_S32a"), PB("r_S32b")]
        if g == 0:
            for eng in (pe, act, dve, sp):
                P.barrier_wait(eng, [b for n, b in REG.items() if n.startswith("m_")])
        gC = [float(np.exp(np.float32(np.log1p(-np.exp2(np.float32(-5.0 - h)))) * np.float32(RC))) for h in range(RH)]

        for h in range(RHL):
            Sh, ShB = S32[h % 2], S32B[h % 2]
            if g == 0:
                V(lambda e, Sh=Sh: e.memset(Sh[:, :, :], 0.0), R=[], W=[ShB])
            else:
                P.dma(sp, Sh[:, :, :], ssc_d[h], R=[ssc_bufs[h]], W=[ShB])
            A(lambda e, Sh=Sh: e.activation(Sbf[:, :, :], Sh[:, :, :], AF.Copy), R=[ShB], W=[SbfB])

            for which, base, dst, dstB in ((0, h * 256, qT, qTB), (1, 1024 + h * 256, kT, kTB)):
                slab, sB = get_slab_t(w_in, base // 256, KC, 256)
                p0, b0 = acc()
                mm(p0[:, :], b0, [(slab[:, kc, 0:128], hT[:, kc, :]) for kc in range(KC)], R=[sB] + hTB)
                p1, b1 = acc()
                mm(p1[:, :], b1, [(slab[:, kc, 128:256], hT[:, kc, :]) for kc in range(KC)], R=[sB] + hTB)
                V(lambda e, p0=p0: e.tensor_tensor(t1, p0[:, :], cosT, ALU.mult), R=[b0, tabsB], W=[tmpB[0]])
                V(lambda e, p1=p1: e.tensor_tensor(t2, p1[:, :], sinT, ALU.mult), R=[b1, tabsB], W=[tmpB[1]])
                V(lambda e, dst=dst: e.tensor_tensor(dst[:, 0, :], t1, t2, ALU.subtract), R=tmpB, W=[dstB])
                V(lambda e, p0=p0: e.tensor_tensor(t1, p0[:, :], sinT, ALU.mult), R=[b0, tabsB], W=[tmpB[0]])
                V(lambda e, p1=p1: e.tensor_tensor(t2, p1[:, :], cosT, ALU.mult), R=[b1, tabsB], W=[tmpB[1]])
                V(lambda e, dst=dst: e.tensor_tensor(dst[:, 1, :], t1, t2, ALU.add), R=tmpB, W=[dstB])
            for which, base, dst, dstBs in ((0, 2048 + h * 512, vT, vTB), (1, 4096 + h * 512, gT, gTB)):
                for half in range(2):
                    slab, sB = get_slab_t(w_in, base // 256 + half, KC, 256)
                    for tt in range(4):
                        p0, b0 = acc()
                        mm(p0[:, 0:256], b0, [(hT[:, kc, tt * 128:(tt + 1) * 128], slab[:, kc, :]) for kc in range(KC)],
                           R=[sB] + hTB)
                        fn = AF.Copy if which == 0 else AF.Silu
                        A(lambda e, p0=p0, dst=dst, tt=tt, half=half, fn=fn:
                          e.activation(dst[:, tt, half * 256:(half + 1) * 256], p0[:, 0:256], fn),
                          R=[b0], W=[dstBs[tt]])
            for c in range(4):
                tok = slice(c * 128, (c + 1) * 128)
                sP, sBk = acc()
                mm(sP[:, 0:128], sBk, [(kT[:, dc, tok], qT[:, dc, tok]) for dc in range(2)], R=[kTB, qTB])
                V(lambda e, sP=sP, h=h: e.tensor_tensor(sTb, sP[:, 0:128], DinT[:, h, :], ALU.mult),
                  R=[sBk, constB], W=[sTbB])
                iaP, iaB = acc()
                mm(iaP[:, :], iaB, [(sTb, vT[:, c, :])], R=[sTbB, vTB[c]])
                ieP, ieB = acc()
                mm(ieP[:, :], ieB, [(qT[:, dc, tok], Sbf[:, dc, :]) for dc in range(2)], R=[qTB, SbfB])
                A(lambda e, ieP=ieP, h=h: e.activation(i32, ieP[:, :], AF.Copy, scale=dq_c[:, h:h + 1]),
                  R=[ieB, constB], W=[i32B])
                V(lambda e, iaP=iaP: e.tensor_tensor(y32, i32, iaP[:, :], ALU.add), R=[i32B, iaB], W=[y32B])
                V(lambda e: e.bn_stats(st6, y32), R=[y32B], W=[smallB])
                V(lambda e: e.bn_aggr(mv, st6), R=[smallB], W=[smallB])
                A(lambda e: e.activation(rs, mv[:, 1:2], AF.Sqrt, bias=EPS, scale=1.0), R=[smallB], W=[smallB])
                V(lambda e: e.reciprocal(rs, rs), R=[smallB], W=[smallB])
                V(lambda e: e.tensor_scalar(y32, y32, mv[:, 0:1], rs, ALU.subtract, ALU.mult),
                  R=[y32B, smallB], W=[y32B])
                V(lambda e, c=c: e.tensor_tensor(yg, y32, gT[:, c, :], ALU.mult), R=[y32B, gTB[c]], W=[ygB])
                for jj in range(4):
                    tr(tpb[:, jj * 128:(jj + 1) * 128], tpA_B, yg[:, jj * 128:(jj + 1) * 128], R=[ygB])
                for jj in range(4):
                    ch = h * 4 + jj
                    V(lambda e, jj=jj, ch=ch, tok=tok: e.tensor_scalar(
                        big[:, ch, tok], tpb[:, jj * 128:(jj + 1) * 128], gn[:, j * 16 + ch:j * 16 + ch + 1], None, ALU.mult),
                      R=[tpA_B, constB], W=[bigB[ch]])
                for dc in range(2):
                    tr(tp2[:, dc * 128:(dc + 1) * 128], a1B, kT[:, dc, tok], R=[kTB])
                V(lambda e, h=h: e.tensor_scalar(ktok, tp2[:, 0:256], dk_c[:, h:h + 1], None, ALU.mult),
                  R=[a1B, constB], W=[ktokB])
                for dc in range(2):
                    kvP, kvB = acc()
                    mm(kvP[:, :], kvB, [(ktok[:, dc * 128:(dc + 1) * 128], vT[:, c, :])], R=[ktokB, vTB[c]])
                    V(lambda e, kvP=kvP, dc=dc, Sh=Sh, h=h: e.scalar_tensor_tensor(
                        Sh[:, dc, :], Sh[:, dc, :], gC_c[:, h:h + 1], kvP[:, :], ALU.mult, ALU.add),
                      R=[ShB, kvB, constB], W=[ShB])
                if c < 3:
                    A(lambda e, Sh=Sh: e.activation(Sbf[:, :, :], Sh[:, :, :], AF.Copy), R=[ShB], W=[SbfB])
            if g < ngroups - 1:
                P.dma(sp, ssc_d[h], Sh[:, :, :], R=[ShB], W=[ssc_bufs[h]], sembuf=ShB)
        w_out = ret_w_out_d[j]
        for jo2 in range(KC // 2):
            slab, sB = get_slab_t(w_out, jo2, KC, 256)
            for sub in range(2):
                jo = jo2 * 2 + sub
                oP, oB = acc()
                mm(oP[:, :], oB, [(slab[:, kc, sub * 128:(sub + 1) * 128], big[:, kc, :]) for kc in range(16)],
                   R=[sB] + bigB[0:16])
                evac_out(jo, oP, oB)

    ckvT = arena[:, 0:4 * S].rearrange("p (c t) -> p c t", c=4)
    kropeT = arena[:, 4 * S:5 * S]
    ckvB = [PB("m_ckv%d" % i) for i in range(NG)]
    kropeB = [PB("m_krope%d" % i) for i in range(NG)]

    def mla(l, g):
        j = l // 2
        o = [5 * S]

        def carve(n_bf16):
            a = arena[:, o[0]:o[0] + n_bf16]
            o[0] += n_bf16
            return a
        cqT = carve(4 * G).rearrange("p (c t) -> p c t", c=4)
        qnT = carve(G)
        qrT = carve(G)
        KhT = carve(S)
        Vh = carve(16 * 128).rearrange("p (k v) -> p k v", k=16)
        PT = carve(2 * G).rearrange("p (k t) -> p k t", k=2)
        rec = carve(2 * G).bitcast(F32)
        assert o[0] <= 40 * 512
        cqB, qnB, qrB, KhB, VhB, recB = PB("m_cq"), PB("m_qn"), PB("m_qr"), PB("m_Kh"), PB("m_Vh"), PB("m_rec")
        PTB = [PB("m_PT0"), PB("m_PT1")]
        if g == 0:
            for eng in (pe, act, dve, sp):
                P.barrier_wait(eng, [b for n, b in REG.items() if n.startswith("r_")])
        w_in = mla_w_in_d[j]
        scale = float((MN + MR) ** -0.5)
        for t in range(8):
            if t % 2 == 0:
                slab, sB = get_slab_t(w_in, t // 2, KC, 256)
            off = (t % 2) * 128
            p0, b0 = acc()
            mm(p0[:, :], b0, [(slab[:, kc, off:off + 128], hT[:, kc, :]) for kc in range(KC)], R=[sB] + hTB)
            A(lambda e, t=t, p0=p0: e.activation(oT_ap(8 + t), p0[:, :], AF.Copy), R=[b0], W=oT_bufs(8 + t))
            if t < 4:
                sq_accum(p0[:, :], b0, t, t == 0, t == 3, ssP, ssB)
            else:
                sq_accum(p0[:, :], b0, t, t == 4, t == 7, a1P, a1B)
        stats_rstd(MQL, 0)
        stats_rstd(MKL, 1, a1P, a1B)
        for c in range(4):
            gi = (j * 2 + 0) * 4 + c
            V(lambda e, c=c, gi=gi: e.scalar_tensor_tensor(cqT[:, c, :], oT_ap(8 + c), gqkv[:, gi:gi + 1],
                                                           rstd[:, 0, :], ALU.mult, ALU.mult),
              R=oT_bufs(8 + c) + [rstdB[0], constB], W=[cqB])
            gi2 = (j * 2 + 1) * 4 + c
            V(lambda e, c=c, gi2=gi2: e.scalar_tensor_tensor(ckvT[:, c, g * G:(g + 1) * G], oT_ap(12 + c),
                                                             gqkv[:, gi2:gi2 + 1], rstd[:, 1, :], ALU.mult, ALU.mult),
              R=oT_bufs(12 + c) + [rstdB[1], constB], W=[ckvB[g]])
        slab, sB = get_slab_t(w_in, 4, KC, 256)
        pA, bA = acc()
        mm(pA[0:64, :], bA, [(slab[:, kc, 0:64], hT[:, kc, :]) for kc in range(KC)], R=[sB] + hTB)
        pB, bB = acc()
        mm(pB[0:64, :], bB, [(slab[:, kc, 64:128], hT[:, kc, :]) for kc in range(KC)], R=[sB] + hTB)
        V(lambda e, pA=pA: e.tensor_tensor(t1[0:64, :], pA[0:64, :], cosT[0:64, :], ALU.mult), R=[bA, tabsB], W=[tmpB[0]])
        V(lambda e, pB=pB: e.tensor_tensor(t2[0:64, :], pB[0:64, :], sinT[0:64, :], ALU.mult), R=[bB, tabsB], W=[tmpB[1]])
        V(lambda e: e.tensor_tensor(kropeT[0:64, g * G:(g + 1) * G], t1[0:64, :], t2[0:64, :], ALU.add),
          R=tmpB, W=[kropeB[g]])
        nkt = 4 * (g + 1)
        for h in range(MHL):
            slq, sqB_ = get_slab_t(mla_w_uq_d[j], h, 4, 256)
            p0, b0 = acc()
            mm(p0[:, :], b0, [(slq[:, kc, 0:128], cqT[:, kc, :]) for kc in range(4)], R=[sqB_, cqB])
            A(lambda e, p0=p0: e.activation(qnT, p0[:, :], AF.Copy), R=[b0], W=[qnB])
            pA, bA = acc()
            mm(pA[0:64, :], bA, [(slq[:, kc, 128:192], cqT[:, kc, :]) for kc in range(4)], R=[sqB_, cqB])
            pB, bB = acc()
            mm(pB[0:64, :], bB, [(slq[:, kc, 192:256], cqT[:, kc, :]) for kc in range(4)], R=[sqB_, cqB])
            V(lambda e, pA=pA: e.tensor_tensor(t1[0:64, :], pA[0:64, :], cosT[0:64, :], ALU.mult), R=[bA, tabsB], W=[tmpB[0]])
            V(lambda e, pB=pB: e.tensor_tensor(t2[0:64, :], pB[0:64, :], sinT[0:64, :], ALU.mult), R=[bB, tabsB], W=[tmpB[1]])
            V(lambda e: e.tensor_tensor(qrT[0:64, :], t1[0:64, :], t2[0:64, :], ALU.add), R=tmpB, W=[qrB])
            slkv, skvB = get_slab_t(mla_w_ukv_d[j], h, 4, 256)
            for kg in range(g + 1):
                p0, b0 = acc()
                mm(p0[:, :], b0, [(slkv[:, kc, 0:128], ckvT[:, kc, kg * G:(kg + 1) * G]) for kc in range(4)],
                   R=[skvB, ckvB[kg]])
                A(lambda e, p0=p0, kg=kg: e.activation(KhT[:, kg * G:(kg + 1) * G], p0[:, :], AF.Copy), R=[b0], W=[KhB])
                p1, b1 = acc()
                for kt in range(4):
                    ktg = kg * 4 + kt
                    mm(p1[:, kt * 128:(kt + 1) * 128], b1,
                       [(ckvT[:, kc, ktg * 128:(ktg + 1) * 128], slkv[:, kc, 128:256]) for kc in range(4)],
                       R=[skvB, ckvB[kg]])
                V(lambda e, p1=p1, kg=kg: e.tensor_copy(
                    Vh[:, kg * 4:(kg + 1) * 4, :], p1[:, :].rearrange("p (k v) -> p k v", k=4)), R=[b1], W=[VhB])
            for kt in range(nkt):
                jd = kt - 4 * g
                kgi = kt // 4
                sP, sBk = acc()
                mm(sP[:, :], sBk, [(KhT[:, kt * 128:(kt + 1) * 128], qnT),
                                   (kropeT[0:64, kt * 128:(kt + 1) * 128], qrT[0:64, :])],
                   R=[KhB, qnB, qrB, kropeB[kgi]])
                k = kt % 2
                A(lambda e, sP=sP, k=k: e.activation(PT[:, k, :], sP[:, :], AF.Exp, scale=scale), R=[sBk], W=[PTB[k]])
                if jd >= 0:
                    V(lambda e, k=k, jd=jd: e.tensor_tensor(PT[:, k, :], PT[:, k, :], maskF[:, jd, :], ALU.mult),
                      R=[PTB[k], constB], W=[PTB[k]])
                mm(a1P[:, :], a1B, [(Vh[:, kt, :], PT[:, k, :])], R=[VhB, PTB[k]], start=(kt == 0), stop=(kt == nkt - 1))
                mm(a2P[:, :], a2B, [(ones_b, PT[:, k, :])], R=[constB, PTB[k]], start=(kt == 0), stop=(kt == nkt - 1))
            V(lambda e: e.reciprocal(rec, a2P[:, :]), R=[a2B], W=[recB])
            V(lambda e, h=h: e.tensor_tensor(big[:, h, :], a1P[:, :], rec, ALU.mult), R=[a1B, recB], W=[bigB[h]])
        w_out = mla_w_out_d[j]
        for jo2 in range(KC // 2):
            slab, sB = get_slab_t(w_out, jo2, MHL, 256)
            for sub in range(2):
                jo = jo2 * 2 + sub
                oP, oB = acc()
                mm(oP[:, :], oB, [(slab[:, kc, sub * 128:(sub + 1) * 128], big[:, kc, :]) for kc in range(MHL)],
                   R=[sB] + bigB[0:MHL])
                evac_out(jo, oP, oB)

    all_arena = [Buf("arena_guard")]
    for l in range(nlayers):
        is_mla = (l % 2 == 1)
        if dbg == 'allret':
            is_mla = False
        if dbg == 'allmla':
            is_mla = True
        for g in range(ngroups):
            src = xT_d if l == 0 else xs_d
            srcB = [] if l == 0 else [xs_bufs[g]]
            P.dma(sp, xg[:, :, :], src[:, :, g * G:(g + 1) * G].rearrange("c p t -> p c t"),
                  R=srcB, W=xgB)
            make_tables(g, is_mla)
            pre_norm(0, l)
            if is_mla:
                mla(l, g)
            else:
                retention(l, g)
            post_norm(1, l)
            pre_norm(2, l)
            ffn(l)
            post_norm(3, l)
            last = (l == nlayers - 1)
            dst = outT_d if last else xs_d
            dstB = out_bufs[g] if last else xs_bufs[g]
            P.dma(sp, dst[:, :, g * G:(g + 1) * G].rearrange("c p t -> p c t"), xg[:, :, :], R=xgB, W=[dstB], sembuf=xgB[0])
    P.barrier_wait(sp, out_bufs)

    sems = {}
    for e in P.engs:
        sems[e.name] = es.enter_context(nc.semaphore("s_" + e.name))
    for k in P.dma_sems:
        sems[k] = es.enter_context(nc.semaphore("s_" + k))
    block = es.enter_context(nc.Block())

    def replay(eng, handle):
        for it in eng.q:
            if it[0] == "wait":
                handle.wait_ge(sems[it[1]], it[2])
            elif it[0] == "op":
                ins = it[1](handle)
                ins.then_inc(sems[eng.name], 1)
            else:
                handle.dma_start(out=it[1], in_=it[2]).then_inc(sems[it[3]], 16)

    @block.tensor
    def _(e):
        replay(pe, e)

    @block.scalar
    def _(e):
        replay(act, e)

    @block.vector
    def _(e):
        replay(dve, e)

    @block.gpsimd
    def _(e):
        replay(pool, e)

    @block.sync
    def _(e):
        replay(sp, e)

    es.close()
    return nc


def _consts(par):
    f32 = np.float32
    hh = (np.arange(RHL) + par * RHL).astype(f32)
    log_gamma = np.log1p(-np.exp2(-5.0 - hh)).astype(f32)
    idx = np.arange(RC, dtype=f32)
    k = idx[:, None]
    q = idx[None, :]
    rel = q - k
    DinT = np.where(rel[None] >= 0, np.exp(log_gamma[:, None, None] * np.maximum(rel, 0)[None]), 0.0).astype(f32) / 16.0
    dq = np.exp(log_gamma[None, :] * (idx[:, None] + 1.0)).astype(f32)
    dk = (np.exp(log_gamma[None, :] * (RC - 1.0 - idx[:, None])) / 16.0).astype(f32)
    gC = np.exp(log_gamma * f32(RC)).astype(f32)
    p = np.arange(128)
    inv_ret = (THETA ** (-(np.arange(0, 256, 2, dtype=f32)) / f32(256))).astype(f32)
    inv32 = (THETA ** (-(np.arange(0, 64, 2, dtype=f32)) / f32(64))).astype(f32)
    inv_mla = inv32[p % 32]
    sgn = np.where((p % 64) < 32, -1.0, 1.0).astype(f32)
    cf = np.zeros((128, 8 * 128 + 8 + 8 + 4 + 128), f32)
    cf[:, 0:512] = DinT.transpose(1, 0, 2).reshape(128, 512)
    cf[:, 512:516] = gC[None, :]
    cf[:, 1024:1028] = dq
    cf[:, 1032:1036] = dk
    cf[:, 1040] = inv_ret / f32(2 * np.pi)
    cf[:, 1041] = inv_mla / f32(2 * np.pi)
    cf[:, 1042] = sgn
    cf[:, 1044:1044 + 128] = 1.0
    cb = np.zeros((128, 128 + 128 + 2048), f32)
    cb[:, 0:128] = np.eye(128)
    cb[:, 128:256] = 1.0
    kk = np.arange(128)[:, None]
    qq = np.arange(512)[None, :]
    for jd in range(4):
        cb[:, 256 + jd * 512:256 + (jd + 1) * 512] = (qq >= jd * 128 + kk)
    return cf, cb.astype(ml_dtypes.bfloat16)


def _tile(w, ncols):
    lw, k, n = w.shape
    kc, ns = k // 128, n // ncols
    return np.ascontiguousarray(w.reshape(lw, kc, 128, ns, ncols).transpose(0, 3, 2, 1, 4).reshape(lw, ns, 128, kc * ncols))


def _fm(v):
    return np.ascontiguousarray(v.reshape(-1, 128).T)


_NC_CACHE = {}


def kernel(x, positions, norm_mix_pre, norm_mix_post, norm_ffn_pre, norm_ffn_post,
           ret_w_in, ret_gn_g, ret_w_out,
           mla_w_in, mla_g_q, mla_g_kv, mla_w_uq, mla_w_ukv, mla_w_out,
           ffn_w_gu, ffn_w_down):
    f32 = np.float32
    x = np.asarray(x, f32)
    c = np.ascontiguousarray
    gains = np.concatenate([_fm(np.asarray(a, f32)[l]) for a in (norm_mix_pre, norm_mix_post, norm_ffn_pre, norm_ffn_post)
                            for l in range(L)], axis=1)
    gqkv = np.concatenate([_fm(np.asarray(a, f32)[j]) for j in range(2) for a in (mla_g_q, mla_g_kv)], axis=1)
    mla_w_in = np.asarray(mla_w_in, f32)
    w_in_ext = c(np.concatenate([mla_w_in, mla_w_in[:, :, 1056:1088], mla_w_in[:, :, 1024:1056]], axis=2))
    wq = np.asarray(mla_w_uq, f32).reshape(2, MQL, MH, MN + MR)
    wq_ext = np.concatenate([wq, wq[..., MN + 32:MN + 64], wq[..., MN:MN + 32]], axis=3)
    ret_w_in = np.asarray(ret_w_in, f32)
    ret_w_out = np.asarray(ret_w_out, f32)
    ret_gn_g = np.asarray(ret_gn_g, f32)
    mla_w_ukv = np.asarray(mla_w_ukv, f32).reshape(2, MKL, MH, 256)
    mla_w_out = np.asarray(mla_w_out, f32)
    ffn_w_gu = np.asarray(ffn_w_gu, f32)
    ffn_w_down = np.asarray(ffn_w_down, f32)
    per_par = []
    for par in range(TP):
        cf, cb = _consts(par)
        hs = slice(par * RHL, (par + 1) * RHL)
        q_ = ret_w_in[:, :, 0:2048].reshape(2, D, RH, 256)[:, :, hs].reshape(2, D, -1)
        k_ = ret_w_in[:, :, 2048:4096].reshape(2, D, RH, 256)[:, :, hs].reshape(2, D, -1)
        v_ = ret_w_in[:, :, 4096:8192].reshape(2, D, RH, 512)[:, :, hs].reshape(2, D, -1)
        g_ = ret_w_in[:, :, 8192:12288].reshape(2, D, RH, 512)[:, :, hs].reshape(2, D, -1)
        ms = slice(par * MHL, (par + 1) * MHL)
        hid = slice(par * (DFF // TP), (par + 1) * (DFF // TP))
        per_par.append({
            "cf": cf, "cb": cb,
            "gn": c(np.concatenate([_fm(ret_gn_g[j][par * 2048:(par + 1) * 2048]) for j in range(2)], axis=1)),
            "ret_w_in": _tile(np.concatenate([q_, k_, v_, g_], axis=2), 256),
            "ret_w_out": _tile(ret_w_out[:, par * 2048:(par + 1) * 2048, :], 256),
            "mla_w_uq": _tile(wq_ext[:, :, ms].reshape(2, MQL, -1), 256),
            "mla_w_ukv": _tile(mla_w_ukv[:, :, ms].reshape(2, MKL, -1), 256),
            "mla_w_out": _tile(mla_w_out[:, par * 1024:(par + 1) * 1024, :], 256),
            "ffn_w_gu": _tile(np.concatenate([ffn_w_gu[:, :, 0:DFF][:, :, hid].reshape(L, D, NFL, 128),
                                               ffn_w_gu[:, :, DFF:][:, :, hid].reshape(L, D, NFL, 128)], axis=3).reshape(L, D, NFL * 256), 256),
            "ffn_w_down": _tile(ffn_w_down[:, hid, :], 128),
        })
    w_in_pad = np.concatenate([w_in_ext, np.zeros((2, D, 128), f32)], axis=2)
    shared = {"gains": c(gains), "gqkv": c(gqkv), "mla_w_in": _tile(w_in_pad, 256)}
    in_maps = []
    for core in range(NCORE):
        b, par = core // TP, core % TP
        m = dict(shared)
        m.update(per_par[par])
        m["xT"] = c(x[b].T).reshape(KC, 128, S)
        m["pos"] = c(np.asarray(positions)[b].astype(np.int32).reshape(1, S))
        in_maps.append(m)
    if "nc" not in _NC_CACHE:
        _NC_CACHE["nc"] = build()
    res = run_bass_kernel_spmd(_NC_CACHE["nc"], in_maps, core_ids=list(range(NCORE)))
    out = np.stack([np.asarray(res.results[b * TP]["outT"]).reshape(D, S).T for b in range(NB)], axis=0)
    return np.ascontiguousarray(out.astype(f32))
```
